# Optimizing a Trainium2 kernel written in Bass

```python
import jax, jax.numpy as jnp
from jax import lax
import numpy as np

D_MODEL = 2048
BATCH = 4
SEQ = 2048
DEPTH = 1

HEAD_DIM = 128
GRID_W = 64
NORM_EPS = 1e-6
Q_BLOCK = 128

A_Q_HEADS = 8
A_KV_HEADS = 2
A_WIDTH = A_Q_HEADS * HEAD_DIM
A_KV_WIDTH = A_KV_HEADS * HEAD_DIM
AXIAL_THETA = 10000.0

B_PATTERNS = ((128, 1), (512, 4), (2048, 16))
B_GROUPS = len(B_PATTERNS)
B_HEADS_PER_GROUP = 4
B_WIDTH = B_HEADS_PER_GROUP * HEAD_DIM
PARTIAL_ROPE_DIM = HEAD_DIM // 4
ROPE_THETA = 500000.0

IN_SIZES = (A_WIDTH, A_KV_WIDTH, A_KV_WIDTH, A_WIDTH,
            B_GROUPS * B_WIDTH, B_GROUPS * B_WIDTH, B_GROUPS * B_WIDTH, B_WIDTH,
            D_MODEL, D_MODEL)
IN_COLS = sum(IN_SIZES)

kernel_name = "hybrid_gated_grid_gqa_dilated_attention_encoder"


def rms_norm(x, g):
    xf = x.astype(jnp.float32)
    y = xf * lax.rsqrt(jnp.mean(xf * xf, axis=-1, keepdims=True) + NORM_EPS)
    return (y * g.astype(jnp.float32)).astype(x.dtype)


def rope_angles(pos, dim, theta):
    expo = jnp.arange(0, dim, 2, dtype=jnp.float32) / dim
    inv_freq = 1.0 / jnp.power(jnp.asarray(theta, jnp.float32), expo)
    ang = pos.astype(jnp.float32)[:, None] * inv_freq[None, :]
    return jnp.cos(ang), jnp.sin(ang)


def apply_rotary(x, cos, sin):
    x1, x2 = jnp.split(x, 2, axis=-1)
    c = cos[None, :, None, :].astype(x.dtype)
    s = sin[None, :, None, :].astype(x.dtype)
    return jnp.concatenate([x1 * c - x2 * s, x1 * s + x2 * c], axis=-1)


def axial_rotary(x, row_id, col_id):
    half = HEAD_DIM // 2
    cr, sr = rope_angles(row_id, half, AXIAL_THETA)
    cc, sc = rope_angles(col_id, half, AXIAL_THETA)
    return jnp.concatenate([apply_rotary(x[..., :half], cr, sr),
                            apply_rotary(x[..., half:], cc, sc)], axis=-1)


def partial_rotary(x, pos):
    c, s = rope_angles(pos, PARTIAL_ROPE_DIM, ROPE_THETA)
    return jnp.concatenate([apply_rotary(x[..., :PARTIAL_ROPE_DIM], c, s),
                            x[..., PARTIAL_ROPE_DIM:]], axis=-1)


def grid_gqa_attention(q, k, v):
    B, S, Hq, D = q.shape
    G = Hq // A_KV_HEADS
    nb = S // Q_BLOCK
    scale = D ** -0.5
    qb = q.reshape(B, nb, Q_BLOCK, A_KV_HEADS, G, D).transpose(1, 0, 2, 3, 4, 5)

    def block(q_blk):
        s = jnp.einsum('bqhgd,bkhd->bhgqk', q_blk, k).astype(jnp.float32) * scale
        p = jax.nn.softmax(s, axis=-1).astype(v.dtype)
        return jnp.einsum('bhgqk,bkhd->bqhgd', p, v)

    o = lax.map(block, qb)
    return o.transpose(1, 0, 2, 3, 4, 5).reshape(B, S, Hq * D)


def dilated_band_attention(q, k, v, dilation, half):
    B, S, H, D = q.shape
    L = S // dilation
    C = half
    nblk = -(-L // C)
    Lp = nblk * C
    scale = D ** -0.5

    def by_stride(t):
        return t.reshape(B, L, dilation, H, D).transpose(0, 2, 1, 3, 4)

    qr, kr, vr = by_stride(q), by_stride(k), by_stride(v)
    qr = jnp.pad(qr, ((0, 0), (0, 0), (0, Lp - L), (0, 0), (0, 0)))
    kv_pad = ((0, 0), (0, 0), (C, Lp - L + C), (0, 0), (0, 0))
    kp = jnp.pad(kr, kv_pad).reshape(B, dilation, nblk + 2, C, H, D)
    vp = jnp.pad(vr, kv_pad).reshape(B, dilation, nblk + 2, C, H, D)

    def band(t):
        return jnp.concatenate([t[:, :, :-2], t[:, :, 1:-1], t[:, :, 2:]], axis=3)

    kband, vband = band(kp), band(vp)
    qb = qr.reshape(B, dilation, nblk, C, H, D)

    qi = jnp.arange(C)[:, None]
    kj = jnp.arange(3 * C)[None, :]
    rel = kj - C - qi
    kpos = jnp.arange(nblk)[:, None, None] * C + kj[None] - C
    valid = (jnp.abs(rel) <= half)[None] & (kpos >= 0) & (kpos < L)

    s = jnp.einsum('brnqhe,brnkhe->brnhqk', qb, kband).astype(jnp.float32) * scale
    s = jnp.where(valid[None, None, :, None], s, -1e30)
    m = jnp.max(s, axis=-1, keepdims=True)
    p = jnp.exp(s - m)
    l = jnp.sum(p, axis=-1)
    o = jnp.einsum('brnhqk,brnkhe->brnqhe', p.astype(v.dtype), vband)
    o = o / l.transpose(0, 1, 2, 4, 3)[..., None].astype(o.dtype)
    lse = (m[..., 0] + jnp.log(l)).transpose(0, 1, 2, 4, 3)

    o = o.reshape(B, dilation, Lp, H, D)[:, :, :L].transpose(0, 2, 1, 3, 4).reshape(B, S, H, D)
    lse = lse.reshape(B, dilation, Lp, H)[:, :, :L].transpose(0, 2, 1, 3).reshape(B, S, H)
    return o, lse


def setup_inputs(seed: int = 0) -> dict:
    key = jax.random.key(seed)
    ks = jax.random.split(key, 10)
    f32 = jnp.float32
    x = jax.random.normal(ks[0], (BATCH, SEQ, D_MODEL), f32)
    norm_gain = 1.0 + 0.02 * jax.random.normal(ks[1], (DEPTH, D_MODEL), f32)
    w_in = jax.random.normal(ks[2], (DEPTH, D_MODEL, IN_COLS), f32) * D_MODEL ** -0.5
    q_norm_gain = 1.0 + 0.02 * jax.random.normal(ks[3], (DEPTH, HEAD_DIM), f32)
    k_norm_gain = 1.0 + 0.02 * jax.random.normal(ks[4], (DEPTH, HEAD_DIM), f32)
    merge_gate_bias = 0.01 * jax.random.normal(ks[5], (DEPTH, 2, D_MODEL), f32)
    w_branch_a = jax.random.normal(ks[6], (DEPTH, A_WIDTH, D_MODEL), f32) * A_WIDTH ** -0.5
    w_branch_b = jax.random.normal(ks[7], (DEPTH, B_WIDTH, D_MODEL), f32) * B_WIDTH ** -0.5
    w_out = jax.random.normal(ks[8], (DEPTH, D_MODEL, D_MODEL), f32) * D_MODEL ** -0.5
    final_norm_gain = 1.0 + 0.02 * jax.random.normal(ks[9], (D_MODEL,), f32)
    return {"x": x, "norm_gain": norm_gain, "w_in": w_in, "q_norm_gain": q_norm_gain,
            "k_norm_gain": k_norm_gain, "merge_gate_bias": merge_gate_bias,
            "w_branch_a": w_branch_a, "w_branch_b": w_branch_b, "w_out": w_out,
            "final_norm_gain": final_norm_gain}


def reference(x, norm_gain, w_in, q_norm_gain, k_norm_gain, merge_gate_bias,
              w_branch_a, w_branch_b, w_out, final_norm_gain):
    B, S, _ = x.shape
    rows = S // GRID_W
    row_grid, col_grid = jnp.meshgrid(jnp.arange(rows, dtype=jnp.int32),
                                      jnp.arange(GRID_W, dtype=jnp.int32), indexing='ij')
    row_id, col_id = row_grid.reshape(-1), col_grid.reshape(-1)
    pos = jnp.arange(S, dtype=jnp.int32)
    split_at = [int(c) for c in np.cumsum(IN_SIZES)[:-1]]

    for l in range(DEPTH):
        h = rms_norm(x, norm_gain[l])
        proj = jnp.einsum('bsd,dc->bsc', h, w_in[l])
        qa, ka, va, ga, qb, kb, vb, gb, za, zb = jnp.split(proj, split_at, axis=-1)

        qa = axial_rotary(rms_norm(qa.reshape(B, S, A_Q_HEADS, HEAD_DIM), q_norm_gain[l]), row_id, col_id)
        ka = axial_rotary(rms_norm(ka.reshape(B, S, A_KV_HEADS, HEAD_DIM), k_norm_gain[l]), row_id, col_id)
        va = va.reshape(B, S, A_KV_HEADS, HEAD_DIM)
        ya = grid_gqa_attention(qa, ka, va) * jax.nn.silu(ga)
        pa = jnp.einsum('bsc,cd->bsd', ya, w_branch_a[l])

        nh = B_GROUPS * B_HEADS_PER_GROUP
        qb = partial_rotary(qb.reshape(B, S, nh, HEAD_DIM), pos).reshape(B, S, B_GROUPS, B_HEADS_PER_GROUP, HEAD_DIM)
        kb = partial_rotary(kb.reshape(B, S, nh, HEAD_DIM), pos).reshape(B, S, B_GROUPS, B_HEADS_PER_GROUP, HEAD_DIM)
        vb = vb.reshape(B, S, B_GROUPS, B_HEADS_PER_GROUP, HEAD_DIM)
        outs, lses = [], []
        for g, (window, dilation) in enumerate(B_PATTERNS):
            o_g, lse_g = dilated_band_attention(qb[:, :, g], kb[:, :, g], vb[:, :, g],
                                                dilation, window // (2 * dilation))
            outs.append(o_g)
            lses.append(lse_g)
        wts = jax.nn.softmax(jnp.stack(lses, axis=0), axis=0)
        ob = jnp.sum(wts[..., None].astype(x.dtype) * jnp.stack(outs, axis=0), axis=0)
        yb = ob.reshape(B, S, B_WIDTH) * jax.nn.silu(gb)
        pb = jnp.einsum('bsc,cd->bsd', yb, w_branch_b[l])

        merged = (jax.nn.sigmoid(za + merge_gate_bias[l, 0]) * pa
                  + jax.nn.sigmoid(zb + merge_gate_bias[l, 1]) * pb)
        x = x + jnp.einsum('bsd,de->bse', merged, w_out[l])

    return rms_norm(x, final_norm_gain)
```

```python
from contextlib import ExitStack
from functools import partial
import numpy as np
import concourse.bass as bass
import concourse.mybir as mybir
from concourse.bass_utils import run_bass_kernel_spmd

F32 = mybir.dt.float32
BF16 = mybir.dt.bfloat16
AF = mybir.ActivationFunctionType
ALU = mybir.AluOpType

D = 2048
T = 2048
TO = 1024
IN_COLS = 11776
QA0, KA0, VA0, GA0, QB0, KB0, VB0, GB0, ZA0, ZB0 = 0, 1024, 1280, 1536, 2560, 4096, 5632, 7168, 7680, 9728
SCALE = float(128 ** -0.5)
EPS = 1e-6
NSLOT = 8
ENGS = ("pe", "act", "dve", "pool", "sp")


class Sched:
    def __init__(self, nc, es):
        self.nc = nc
        self.es = es
        self.sem = {e: es.enter_context(nc.semaphore("s_" + e)) for e in ENGS}
        self.cnt = {e: 0 for e in ENGS}
        self.prog = {e: [] for e in ENGS}
        self.waited = {}
        self.lastw = {}
        self.readers = {}
        self.dmasem = {}
        self.stopped = False

    def _deps(self, eng, reads, writes, is_dma):
        toks = []
        for k in reads:
            t = self.lastw.get(k)
            if t is not None:
                toks.append(t)
            if isinstance(k, tuple) and k[0] == "ps":
                for r in self.readers.get(k, ()):
                    if r[2] != eng:
                        toks.append(r)
        for k in writes:
            t = self.lastw.get(k)
            if t is not None and (is_dma or t[2] != eng or eng != "pe"):
                toks.append(t)
            for r in self.readers.get(k, ()):
                if is_dma or r[2] != eng or eng != "pe":
                    toks.append(r)
        best = {}
        for t in toks:
            if t[3] not in best or best[t[3]][1] < t[1]:
                best[t[3]] = t
        out = []
        for nm, t in best.items():
            key = (eng, nm)
            if self.waited.get(key, 0) >= t[1]:
                continue
            self.waited[key] = t[1]
            out.append((t[0], t[1]))
        return out

    def _commit(self, tok, reads, writes):
        for k in writes:
            self.lastw[k] = tok
            self.readers[k] = []
        for k in reads:
            self.readers.setdefault(k, []).append(tok)

    def op(self, eng, method, kw, reads=(), writes=(), inc=True):
        if self.stopped:
            return None
        fn = (method, kw)
        waits = self._deps(eng, reads, writes, False)
        if inc:
            self.cnt[eng] += 1
            tok = (self.sem[eng], self.cnt[eng], eng, "s_" + eng)
        else:
            tok = (self.sem[eng], self.cnt[eng] + 1, eng, "s_" + eng)
        self.prog[eng].append((fn, waits, (self.sem[eng], 1) if inc else None))
        self._commit(tok, reads, writes)
        return tok

    def dma(self, eng, out, in_, semname, reads=(), writes=()):
        if self.stopped:
            return None
        fn = ("dma_start", dict(out=out, in_=in_))
        if semname not in self.dmasem:
            self.dmasem[semname] = [self.es.enter_context(self.nc.semaphore("d_" + semname)), 0]
        ent = self.dmasem[semname]
        waits = self._deps(eng, reads, writes, True)
        ent[1] += 16
        tok = (ent[0], ent[1], None, "d_" + semname)
        self.prog[eng].append((fn, waits, (ent[0], 16)))
        self._commit(tok, reads, writes)
        return tok

    def alias(self, new_key, old_keys):
        toks = []
        for k in old_keys:
            if self.lastw.get(k) is not None:
                toks.append(self.lastw[k])
            toks.extend(self.readers.get(k, ()))
        self.readers.setdefault(new_key, []).extend(toks)

    def wait_token(self, eng, tok):
        if self.stopped or tok is None:
            return
        key = (eng, tok[3])
        if self.waited.get(key, 0) >= tok[1]:
            return
        self.waited[key] = tok[1]
        self.prog[eng].append((None, [(tok[0], tok[1])], None))

    def barrier(self):
        if self.stopped:
            return
        toks = []
        for e in ENGS:
            if self.cnt[e] > 0:
                toks.append((self.sem[e], self.cnt[e], e, "s_" + e))
        for nm, ent in self.dmasem.items():
            if ent[1] > 0:
                toks.append((ent[0], ent[1], None, "d_" + nm))
        for e in ENGS:
            for t in toks:
                if t[2] != e:
                    self.wait_token(e, t)

    def check(self):
        semv = {}
        pc = {e: 0 for e in ENGS}
        progress = True
        while progress:
            progress = False
            for e in ENGS:
                prog = self.prog[e]
                while pc[e] < len(prog):
                    fn, waits, inc = prog[pc[e]]
                    if any(semv.get(id(s_), 0) < v for (s_, v) in waits):
                        break
                    if inc is not None:
                        semv[id(inc[0])] = semv.get(id(inc[0]), 0) + inc[1]
                    pc[e] += 1
                    progress = True
        stuck = {e: (pc[e], len(self.prog[e])) for e in ENGS if pc[e] < len(self.prog[e])}
        if stuck:
            msg = []
            for e, (i, n) in stuck.items():
                fn, waits, inc = self.prog[e][i]
                msg.append("%s stuck at %d/%d %s waits=%s" % (e, i, n, fn[0] if fn else None,
                           [(str(s_), v, semv.get(id(s_), 0)) for (s_, v) in waits]))
            raise RuntimeError("DEADLOCK: " + " | ".join(msg))
        return {e: len(self.prog[e]) for e in ENGS}

    def emit(self, block):
        names = {"pe": "tensor", "act": "scalar", "dve": "vector", "pool": "gpsimd", "sp": "sync"}
        for e in ENGS:
            prog = self.prog[e]
            if not prog:
                continue

            def body(engine, prog=prog):
                for fn, waits, inc in prog:
                    for (s, v) in waits:
                        engine.wait_ge(s, v)
                    if fn is not None:
                        ins = getattr(engine, fn[0])(**fn[1])
                        if inc is not None:
                            ins.then_inc(inc[0], inc[1])

            getattr(block, names[e])(body)


def build_nc(debug=False, stop_after=None):
    nc = bass.Bass("TRN2", target_bir_lowering=False)

    def din(name, shape):
        return nc.dram_tensor(name, shape, F32, kind="ExternalInput").ap()

    x = din("x", [T, D])
    w_in = din("w_in", [D, IN_COLS])
    wba_d = din("wba", [1024, D])
    wbb_d = din("wbb", [512, D])
    wout_d = din("wout", [D, D])
    gn_d = din("gn", [128, D])
    gf_d = din("gf", [128, D])
    qkg_d = din("qkg", [128, 2])
    mb_d = din("mbias", [128, 32])
    ropeA_d = din("ropeA", [2, 128, T])
    ropeB_d = din("ropeB", [2, 32, T])
    cbf_d = din("cbf", [128, 1088])
    cf_d = din("cf", [128, 384])
    y = nc.dram_tensor("y", [TO, D], F32, kind="ExternalOutput").ap()
    dbg = None
    if debug:
        dbg = nc.dram_tensor("dbg", [128, 32768], BF16, kind="ExternalOutput").ap()

    es = ExitStack()
    with es:
        es.enter_context(nc.allow_low_precision("bf16 matmul operands, fp32 accumulation"))

        def sb(name, shape, dt):
            return es.enter_context(nc.sbuf_tensor(name, shape, dt))

        hT = sb("s_hT", [128, 16, 2048], BF16)
        wsl = sb("s_wsl", [128, NSLOT, 16, 128], BF16)
        yaT = sb("s_yaT", [128, 8, 1024], BF16)
        ybT = sb("s_ybT", [128, 4, 1024], BF16)
        rope = sb("s_rope", [128, 2, 2048], F32)
        gain = sb("s_gain", [128, 2048], F32)
        arena = sb("s_arena", [128, 13312], F32)
        cbf = sb("s_cbf", [128, 1088], BF16)
        cf = sb("s_cf", [128, 384], F32)
        qkg = sb("s_qkg", [128, 2], F32)
        mb = sb("s_mb", [128, 32], F32)
        eps_t = sb("s_eps_t", [128, 1], F32)
        one_t = sb("s_one_t", [128, 1], F32)
        stat = sb("s_stat", [128, 64], F32)
        pb = [es.enter_context(nc.psum_tensor("pb%d" % i, [128, 512], F32)) for i in range(8)]
        S = Sched(nc, es)

        ident = cbf[:, 0:128]
        ones_bf = cbf[:, 128:256]
        maskB = cbf[:, 384:640]
        mask0 = cbf[:, 640:768]
        mask2 = cbf[:, 768:832]
        permA_bf = cbf[:, 832:960]
        permB_bf = cbf[:, 960:1088]
        permA = cf[:, 0:128]
        permB = cf[:, 128:256]
        meanF = cf[:, 256:384]

        def af(a, b):
            return arena[:, a:b]

        def ab(a, b):
            return arena[:, a:b].bitcast(BF16)

        def PS(i):
            return ("ps", i)

        class _Stop(Exception):
            pass

        def stop_here(name, ap2d, ncols):
            if stop_after != name:
                return
            S.barrier()
            tk = S.dma("sp", dbg[:, 0:ncols], ap2d, "dbgout")
            S.wait_token("sp", tk)
            S.stopped = True

        def MM(out, lhsT, rhs, start, stop, reads, writes, inc=None):
            S.op("pe", "matmul", dict(out=out, lhsT=lhsT, rhs=rhs, start=start, stop=stop), reads, writes,
                 inc=(stop if inc is None else inc))

        def TR(out, in_, reads, writes, inc):
            S.op("pe", "transpose", dict(out=out, in_=in_, identity=ident), list(reads) + ["cbf"], writes, inc=inc)

        def ACT(out, in_, func, reads, writes, **kw):
            S.op("act", "activation", dict(out=out, in_=in_, func=func, **kw), reads, writes)

        def TT(out, in0, in1, op, reads, writes, eng="dve"):
            S.op(eng, "tensor_tensor", dict(out=out, in0=in0, in1=in1, op=op), reads, writes)

        def CP(out, in_, reads, writes, eng="dve"):
            S.op(eng, "tensor_copy", dict(out=out, in_=in_), reads, writes)

        def RCP(out, in_, reads, writes):
            S.op("dve", "reciprocal", dict(out=out, in_=in_), reads, writes)

        def STT(out, in0, scalar, in1, reads, writes):
            S.op("dve", "scalar_tensor_tensor", dict(out=out, in0=in0, scalar=scalar, in1=in1, op0=ALU.mult, op1=ALU.mult),
                 reads, writes)

        def MEMSET(ap, val, writes, eng="pool"):
            S.op(eng, "memset", dict(ap=ap, constant=val), (), writes)

        wlist = []

        def wsrc(dram, c0):
            return dram[:, c0:c0 + 128].rearrange("(k p) c -> p k c", p=128)

        for kvh in range(2):
            wlist.append((("ka", kvh), wsrc(w_in, KA0 + kvh * 128), 16))
            wlist.append((("va", kvh), wsrc(w_in, VA0 + kvh * 128), 16))
        for h in range(8):
            wlist.append((("qa", h), wsrc(w_in, QA0 + h * 128), 16))
            wlist.append((("ga", h), wsrc(w_in, GA0 + h * 128), 16))
        for hs in range(4):
            for g in range(3):
                c = g * 512 + hs * 128
                wlist.append((("kb", g, hs), wsrc(w_in, KB0 + c), 16))
                wlist.append((("vb", g, hs), wsrc(w_in, VB0 + c), 16))
                wlist.append((("qb", g, hs), wsrc(w_in, QB0 + c), 16))
            wlist.append((("gb", hs), wsrc(w_in, GB0 + hs * 128), 16))
        for c in range(16):
            wlist.append((("za", c), wsrc(w_in, ZA0 + c * 128), 16))
            wlist.append((("zb", c), wsrc(w_in, ZB0 + c * 128), 16))
            wlist.append((("wba", c), wsrc(wba_d, c * 128), 8))
            wlist.append((("wbb", c), wsrc(wbb_d, c * 128), 4))
        wstate = {"issued": 0, "used": 0}

        def w_issue():
            i = wstate["issued"]
            if i >= len(wlist):
                return
            tag, src, nk = wlist[i]
            slot = i % NSLOT
            S.dma("pool", wsl[:, slot, 0:nk, :], src, "w%d" % slot, writes=[("w", slot)])
            wstate["issued"] += 1

        def W(tag, lookahead=NSLOT - 1):
            i = wstate["used"]
            assert wlist[i][0] == tag, (wlist[i][0], tag)
            while wstate["issued"] < min(len(wlist), i + 1 + lookahead):
                w_issue()
            wstate["used"] += 1
            slot = i % NSLOT
            return wsl[:, slot], ("w", slot)

        S.dma("sp", gain[:], gn_d, "c_gain", writes=["gain"])
        S.dma("sp", cf[:], cf_d, "c_cf", writes=["cf"])
        S.dma("sp", qkg[:], qkg_d, "c_qkg", writes=["qkg"])
        S.dma("sp", mb[:], mb_d, "c_mb", writes=["mb"])
        S.dma("pool", cbf[:], cbf_d, "c_cbf", writes=["cbf"])
        MEMSET(eps_t[:], EPS, ["eps"])
        MEMSET(one_t[:], 1.0, ["one"])
        MEMSET(stat[:], 0.0, ["stat"])
        for i in range(NSLOT - 1):
            w_issue()
        S.dma("sp", rope[:, 0, :], ropeA_d[0], "c_rope", writes=["rope"])
        S.dma("sp", rope[:, 1, :], ropeA_d[1], "c_rope", writes=["rope"])

        xbs = [yaT[:, 4 * i:4 * i + 4, :].rearrange("p h n -> p (h n)").bitcast(F32) for i in range(2)]
        xbk = [[("yaT", h) for h in range(4 * i, 4 * i + 4)] for i in range(2)]
        xbs += [af(5120, 7168)]
        xbk += [["xb2"]]
        NXB = 3
        hbs = [ybT[:, 2 * i:2 * i + 2, :].rearrange("p h n -> p (h n)") for i in range(2)]
        hbk = [[("ybT", 2 * i), ("ybT", 2 * i + 1)] for i in range(2)]
        junk = ab(10752, 11776)
        jk = ["xb3"]

        def hk(k, t0=None, n=None):
            if t0 is None:
                return [("hT", k, s_) for s_ in range(4)]
            return [("hT", k, s_) for s_ in range(t0 // 512, (t0 + n - 1) // 512 + 1)]

        def p1load(t):
            p4 = t % NXB
            S.dma("sp", xbs[p4], x[t * 128:(t + 1) * 128, :], "x%d" % p4, writes=xbk[p4])

        def p1a(t):
            xb, hb, p2, p4 = xbs[t % NXB], hbs[t % 2], t % 2, t % NXB
            if t + 2 < 16:
                p1load(t + 2)
            ACT(junk, xb, AF.Square, xbk[p4] + ["stat"], jk + [("st", t)], accum_out=stat[:, t:t + 1])
            ACT(stat[:, 16 + t:17 + t], stat[:, t:t + 1], AF.Ln, [("st", t), "eps"], [("sd", t)],
                scale=1.0 / D, bias=eps_t[:, 0:1])
            ACT(stat[:, 32 + t:33 + t], stat[:, 16 + t:17 + t], AF.Exp, [("sd", t)], [("rs", t)], scale=-0.5)
            STT(hb, xb, stat[:, 32 + t:33 + t], gain[:], xbk[p4] + [("rs", t), "gain"], hbk[p2])

        def p1b(t):
            hb, p2 = hbs[t % 2], t % 2
            for half in range(2):
                bi = 6 + half
                bank = pb[bi][:].bitcast(BF16)
                for c in range(8):
                    k = half * 8 + c
                    TR(bank[:, c * 128:(c + 1) * 128], hb[:, k * 128:(k + 1) * 128], hbk[p2], [PS(bi)], inc=(c == 7))
                dst = hT[:, half * 8:(half + 1) * 8, t * 128:(t + 1) * 128]
                src = bank.rearrange("p (c n) -> p c n", c=8)
                wr = [("hT", half * 8 + c, t // 4) for c in range(8)]
                if half == 0:
                    ACT(dst, src, AF.Copy, [PS(bi)], wr)
                else:
                    CP(dst, src, [PS(bi)], wr)

        p1load(0)
        p1load(1)
        p1a(0)
        p1_grp = {1: [], 2: [], 3: []}
        for t in range(16):
            if t < 4:
                p1a(t + 1)
                p1b(t)
            else:
                grp = p1_grp[t // 4]
                if t + 1 < 16:
                    grp.append(partial(p1a, t + 1))
                grp.append(partial(p1b, t))

        def proj_fm(wslot, wkey, nk, src, srckeys, slices, banks):
            for k in range(nk):
                for (t0, n), bi in zip(slices, banks):
                    sk = hk(k, t0, n) if srckeys is None else srckeys
                    MM(pb[bi][:, 0:n], wslot[:, k, :], src[:, k, t0:t0 + n], (k == 0), (k == nk - 1),
                       [wkey] + sk, [PS(bi)])

        hTkeys = None

        kaTs = [ab(0, 1024), ab(1024, 2048)]
        vTst = ab(2048, 3072)
        vatoks = [ab(3072, 4096), ab(4096, 5120)]
        qaTs = [ab(5120, 5632), ab(5632, 6144)]
        sgas = [af(6144, 7168), af(7168, 8192)]
        sq = af(8192, 8704)
        stdt = af(8704, 9216)
        qn = af(9216, 9728)
        t1 = af(9728, 10240)
        t2 = af(10240, 10752)
        pbufs = [ab(10752 + 256 * i, 11008 + 256 * i) for i in range(4)]
        rec = af(11776, 12288)
        tmp = af(12288, 12800)

        sqb = sq.bitcast(BF16)[:, 0:512]
        hiA = ab(12800, 13056)
        loA = ab(13056, 13312)

        def qk_norm_rope_A(bi, gcol, t0, dst, dstkey):
            ps = pb[bi][:]
            ACT(sqb, ps, AF.Square, [PS(bi)], ["sq"])
            MM(pb[2][:], ones_bf, sqb, True, True, ["sq", "cbf"], [PS(2)])
            ACT(stdt, pb[2][:], AF.Ln, [PS(2), "eps"], ["stdt"], scale=1.0 / 128, bias=eps_t[:, 0:1])
            ACT(stdt, stdt, AF.Exp, ["stdt"], ["stdt"], scale=-0.5)
            STT(qn, ps, qkg[:, gcol:gcol + 1], stdt, [PS(bi), "qkg", "stdt"], ["qn"])
            MM(pb[3][:], permA, qn, True, True, ["qn", "cf"], [PS(3)])
            TT(t1, qn, rope[:, 0, t0:t0 + 512], ALU.mult, ["qn", "rope"], ["t1"])
            TT(t2, pb[3][:], rope[:, 1, t0:t0 + 512], ALU.mult, [PS(3), "rope"], ["t2"])
            TT(dst, t1, t2, ALU.add, ["t1", "t2"], [dstkey])

        def interleave(main, side):
            nm, ns = len(main), len(side)
            j = 0
            for i, f in enumerate(main):
                f()
                tgt = ((i + 1) * ns) // nm
                while j < min(tgt, ns):
                    side[j]()
                    j += 1
            while j < ns:
                side[j]()
                j += 1

        def nop():
            pass

        wkv = {}
        for kvh in range(2):
            wkv[("ka", kvh)] = W(("ka", kvh), 4)
            wkv[("va", kvh)] = W(("va", kvh), 4)

        def kv_proj(tag, sl, banks, k0, k1):
            wslot, wkey = wkv[tag]
            for k in range(k0, k1):
                for (t0, n), bi in zip(sl, banks):
                    MM(pb[bi][:, 0:n], wslot[:, k, :], hT[:, k, t0:t0 + n], (k == 0), (k == 15), [wkey] + hk(k, t0, n), [PS(bi)])

        def kv_proj1(tag, t0, bi, k0, k1):
            wslot, wkey = wkv[tag]
            for k in range(k0, k1):
                MM(pb[bi][:, 0:512], wslot[:, k, :], hT[:, k, t0:t0 + 512], (k == 0), (k == 15), [wkey] + hk(k, t0, 512), [PS(bi)])

        def k_units(sl_idx):
            th = []
            t0 = sl_idx * 512
            for kvh in range(2):
                kaT, kkey = kaTs[kvh], ("kaT", kvh)
                bi = kvh
                for k0 in range(0, 16, 4):
                    th.append(partial(kv_proj1, ("ka", kvh), t0, bi, k0, k0 + 4))

                def k_sq(bi=bi):
                    ACT(sqb, pb[bi][:], AF.Square, [PS(bi)], ["sq"])

                def k_ssb():
                    MM(pb[2][:], ones_bf, sqb, True, True, ["sq", "cbf"], [PS(2)])

                def k_sqrt():
                    ACT(stdt, pb[2][:], AF.Ln, [PS(2), "eps"], ["stdt"], scale=1.0 / 128, bias=eps_t[:, 0:1])

                def k_rcp():
                    ACT(stdt, stdt, AF.Exp, ["stdt"], ["stdt"], scale=-0.5)

                def k_stt(bi=bi):
                    STT(qn, pb[bi][:], qkg[:, 1:2], stdt, [PS(bi), "qkg", "stdt"], ["qn"])

                def k_hi():
                    CP(hiA, qn, ["qn"], ["hiA"])

                def k_lo():
                    S.op("dve", "tensor_tensor", dict(out=loA, in0=qn, in1=hiA, op=ALU.subtract), ["qn", "hiA"], ["loA"])

                def k_swap():
                    MM(pb[3][:], permA_bf, hiA, True, False, ["hiA", "cbf"], [PS(3)])
                    MM(pb[3][:], permA_bf, loA, False, True, ["loA", "cbf"], [PS(3)])
                    TT(t1, qn, rope[:, 0, t0:t0 + 512], ALU.mult, ["qn", "rope"], ["t1"])

                def k_t2():
                    TT(t2, pb[3][:], rope[:, 1, t0:t0 + 512], ALU.mult, [PS(3), "rope"], ["t2"])

                def k_add(kaT=kaT, kkey=kkey):
                    TT(kaT[:, t0:t0 + 512], t1, t2, ALU.add, ["t1", "t2"], [kkey])
                th += [k_sq, k_ssb, k_sqrt, k_rcp, k_stt, k_hi, k_lo, k_swap, k_t2, k_add]
            return th

        vTsts = [vTst, af(7168, 8192).bitcast(BF16)]

        def v_units(sl_idx):
            th = []
            t0 = sl_idx * 512
            for kvh in range(2):
                vst, vsk = vTsts[kvh], ("vTst", kvh)
                bi = 4 + kvh
                for k0 in range(0, 16, 4):
                    th.append(partial(kv_proj1, ("va", kvh), t0, bi, k0, k0 + 4))

                def v_ev(bi=bi, vst=vst, vsk=vsk):
                    ACT(vst[:, t0:t0 + 512], pb[bi][:], AF.Copy, [PS(bi)], [vsk])
                th.append(v_ev)
            return th

        def v_transposes():
            th = []
            for kvh in range(2):
                vatok, vkey, vst, vsk = vatoks[kvh], ("vatok", kvh), vTsts[kvh], ("vTst", kvh)
                for half in range(2):
                    def v_tr(half=half, vatok=vatok, vkey=vkey, vst=vst, vsk=vsk):
                        bank = pb[6 + half][:].bitcast(BF16)
                        for c in range(8):
                            ch = half * 8 + c
                            TR(bank[:, c * 128:(c + 1) * 128], vst[:, ch * 128:(ch + 1) * 128], [vsk], [PS(6 + half)], inc=(c == 7))
                        CP(vatok[:, half * 1024:(half + 1) * 1024], bank, [PS(6 + half)], [vkey])
                    th.append(v_tr)
            return th

        def merge2(a, b):
            out = []
            a, b = list(a), list(b)
            while a or b:
                if a:
                    out.append(a.pop(0))
                if b:
                    out.append(b.pop(0))
            return out

        for sidx in range(3):
            interleave(p1_grp[sidx + 1], merge2(k_units(sidx), v_units(sidx)))
        stop_here("p1", hT[:].rearrange("p k n -> p (k n)"), 32768)
        S.alias(("qaT", 0), ["xb2"])
        S.alias(("qaT", 1), ["xb2"])
        S.alias(("sga", 0), ["xb2"])
        for i in range(4):
            S.alias(("pbuf", i), ["xb3"])
        S.alias("rec", ["xb3"])
        S.alias("tmp", ["xb3"])
        PREP0 = []


        slA = [(0, 512), (512, 512)]

        def prepA(h):
            th = []
            qaT, sga = qaTs[h % 2], sgas[h % 2]
            qkey, gkey = ("qaT", h % 2), ("sga", h % 2)
            st = {}

            def getw(tag):
                st[tag] = W(tag, 3) if h == 0 else W(tag)

            def proj_part(tag, k0, k1):
                wslot, wkey = st[tag]
                for k in range(k0, k1):
                    for (t0, n), bi in zip(slA, [0, 1]):
                        MM(pb[bi][:, 0:n], wslot[:, k, :], hT[:, k, t0:t0 + n], (k == 0), (k == 15), [wkey] + hk(k, t0, n), [PS(bi)])

            def c_sq(bi):
                ACT(sqb, pb[bi][:], AF.Square, [PS(bi)], ["sq"])

            def c_ssb():
                MM(pb[2][:], ones_bf, sqb, True, True, ["sq", "cbf"], [PS(2)])

            def c_sqrt():
                ACT(stdt, pb[2][:], AF.Ln, [PS(2), "eps"], ["stdt"], scale=1.0 / 128, bias=eps_t[:, 0:1])

            def c_rcp():
                ACT(stdt, stdt, AF.Exp, ["stdt"], ["stdt"], scale=-0.5)

            def c_stt(bi):
                STT(qn, pb[bi][:], qkg[:, 0:1], stdt, [PS(bi), "qkg", "stdt"], ["qn"])

            def c_hi():
                CP(hiA, qn, ["qn"], ["hiA"])

            def c_lo():
                S.op("dve", "tensor_tensor", dict(out=loA, in0=qn, in1=hiA, op=ALU.subtract), ["qn", "hiA"], ["loA"])

            def c_swap(t0):
                MM(pb[2][:], permA_bf, hiA, True, False, ["hiA", "cbf"], [PS(2)])
                MM(pb[2][:], permA_bf, loA, False, True, ["loA", "cbf"], [PS(2)])
                TT(t1, qn, rope[:, 0, t0:t0 + 512], ALU.mult, ["qn", "rope"], ["t1"])

            def c_t2(t0):
                TT(t2, pb[2][:], rope[:, 1, t0:t0 + 512], ALU.mult, [PS(2), "rope"], ["t2"])

            def c_add(t0):
                TT(qaT[:, t0:t0 + 512], t1, t2, ALU.add, ["t1", "t2"], [qkey])

            def silu(t0, bi):
                dst = sga[:, t0:t0 + 512]
                ACT(dst, pb[bi][:], AF.Exp, [PS(bi)], [gkey], scale=-1.0)
                ACT(dst, dst, AF.Ln, [gkey, "one"], [gkey], bias=one_t[:, 0:1])
                ACT(dst, dst, AF.Exp, [gkey], [gkey], scale=-1.0)
                TT(dst, dst, pb[bi][:], ALU.mult, [gkey, PS(bi)], [gkey])

            th.append(partial(getw, ("qa", h)))
            for k0 in range(0, 16, 2):
                th.append(partial(proj_part, ("qa", h), k0, k0 + 2))
            for (t0, n), bi in zip(slA, [0, 1]):
                th += [partial(c_sq, bi), c_ssb, c_sqrt, nop, c_rcp, nop, nop, partial(c_stt, bi), nop,
                       c_hi, c_lo, nop, partial(c_swap, t0), nop, partial(c_t2, t0), nop, partial(c_add, t0)]
            th.append(partial(getw, ("ga", h)))
            for k0 in range(0, 16, 2):
                th.append(partial(proj_part, ("ga", h), k0, k0 + 2))
            th.append(nop)
            for (t0, n), bi in zip(slA, [0, 1]):
                th.append(partial(silu, t0, bi))
            return th

        def attnA(h):
            th = []
            kvh = h // 4
            kaT, vatok = kaTs[kvh], vatoks[kvh]
            kkey, vkey = ("kaT", kvh), ("vatok", kvh)
            qaT, sga = qaTs[h % 2], sgas[h % 2]
            qkey, gkey = ("qaT", h % 2), ("sga", h % 2)

            SB = [3, 4, 5]

            def s_mm(q0, kc):
                bi = SB[kc % 3]
                MM(pb[bi][:], kaT[:, kc * 128:(kc + 1) * 128], qaT[:, q0:q0 + 512], True, True, [kkey, qkey], [PS(bi)])

            def stepa(q0, kc):
                bi = SB[kc % 3]
                ACT(pbufs[kc % 4], pb[bi][:], AF.Exp, [PS(bi)], [("pbuf", kc % 4)], scale=SCALE)
                if kc + 2 < 16:
                    s_mm(q0, kc + 2)

            def stepb(q0, kc):
                pbuf, pkey = pbufs[kc % 4], ("pbuf", kc % 4)
                MM(pb[6][:], vatok[:, kc * 128:(kc + 1) * 128], pbuf, (kc == 0), (kc == 15), [vkey, pkey], [PS(6)])
                MM(pb[7][:], ones_bf, pbuf, (kc == 0), (kc == 15), ["cbf", pkey], [PS(7)])

            def epi(q0):
                ACT(rec, pb[7][:], AF.Ln, [PS(7)], ["rec"])
                ACT(rec, rec, AF.Exp, ["rec"], ["rec"], scale=-1.0)
                TT(rec, rec, sga[:, q0:q0 + 512], ALU.mult, ["rec", gkey], ["rec"])
                TT(yaT[:, h, q0:q0 + 512], pb[6][:], rec, ALU.mult, [PS(6), "rec"], [("yaT", h)])

            def start(q0):
                s_mm(q0, 0)
                s_mm(q0, 1)

            for qb in range(2):
                q0 = qb * 512
                th.append(partial(start, q0))
                for kc in range(16):
                    th.append(partial(stepa, q0, kc))
                    if kc >= 1:
                        th.append(partial(stepb, q0, kc - 1))
                th.append(partial(stepb, q0, 15))
                th.append(partial(epi, q0))
            return th

        interleave(k_units(3) + prepA(0), v_units(3) + v_transposes())
        S.alias(("sga", 1), [("vTst", 1)])
        S.alias("vTst", [("vTst", 0)])
        for h in range(7):
            interleave(attnA(h), prepA(h + 1))
        kTs = [ab(0, 1024), ab(1024, 2048)]
        vTst = ab(2048, 3072)
        vtoks = [ab(3072, 4096), ab(4096, 5120)]
        qTs = [ab(5120, 5632), ab(5632, 6144)]
        sgb = af(6144, 7168)
        xs = af(8192, 8704)
        U = af(11776, 12800)
        L = af(7168, 8192)
        GD = (1, 4, 16)
        GRL = (1152, 384, 128)
        GPAD = (64, 64, 0)
        GKTOK = (1088, 1280, 2048)

        def slices_of(ntok):
            out = []
            t0 = 0
            while t0 < ntok:
                n = min(512, ntok - t0)
                out.append((t0, n))
                t0 += n
            return out

        def deint(buf, g, t0, n, own):
            d = GD[g]
            if own:
                rl, pad = TO // d, 0
            else:
                rl, pad = GRL[g], GPAD[g]
            if d == 1:
                return buf[:, pad + t0:pad + t0 + n]
            i0, ni = t0 // d, n // d
            return buf[:, 0:d * rl].rearrange("p (r c) -> p r c", r=d)[:, :, pad + i0:pad + i0 + ni]

        def nat(ap, g):
            d = GD[g]
            if d == 1:
                return ap
            return ap.rearrange("p (i r) -> p r i", r=d)

        def qk_rope_B(bi, g, t0, n, buf, bkey, own):
            ACT(deint(buf, g, t0, n, own), nat(pb[bi][:, 0:n], g), AF.Copy, [PS(bi)], [bkey])
            CP(xs[:, 0:n], pb[bi][:, 0:n], [PS(bi)], ["xs"])
            MM(pb[2][:, 0:n], permB, xs[:, 0:n], True, True, ["xs", "cf"], [PS(2)])
            TT(t1[0:32, 0:n], xs[0:32, 0:n], rope[0:32, 0, t0:t0 + n], ALU.mult, ["xs", "rope"], ["t1"])
            TT(t2[0:32, 0:n], pb[2][0:32, 0:n], rope[0:32, 1, t0:t0 + n], ALU.mult, [PS(2), "rope"], ["t2"])
            TT(deint(buf[0:32], g, t0, n, own), nat(t1[0:32, 0:n], g), nat(t2[0:32, 0:n], g), ALU.add,
               ["t1", "t2", bkey], [bkey])

        xs_l = [af(8192, 8704), af(8704, 9216)]
        t2_l = [af(9728, 10240), af(10240, 10752)]
        hiB = [ab(9216, 9472), ab(12800, 13056)]
        loB = [ab(9472, 9728), ab(13056, 13312)]
        PB_SW = 7
        iters = [(hs, g) for hs in range(4) for g in range(3)]

        def prepB(n, nproj=3, PB_SW=7):
            hs, g = iters[n]
            d = GD[g]
            kT, vtok, qT = kTs[n % 2], vtoks[n % 2], qTs[n % 2]
            kkey, vkey, qkey = ("kT", n % 2), ("vtok", n % 2), ("qT", n % 2)
            ksl = slices_of(GKTOK[g])
            th = []
            st = {}
            pairno = [0]

            def getw(tag):
                st[tag] = W(tag)

            def proj_part(tag, sl, banks, k0, k1):
                wslot, wkey = st[tag]
                if k0 == 0:
                    order = [(k, j) for j in range(len(sl)) for k in range(k0, k1)]
                else:
                    order = [(k, j) for k in range(k0, k1) for j in range(len(sl))]
                for k, j in order:
                    (t0, n_), bi = sl[j], banks[j]
                    MM(pb[bi][:, 0:n_], wslot[:, k, :], hT[:, k, t0:t0 + n_], (k == 0), (k == 15), [wkey] + hk(k, t0, n_), [PS(bi)])

            def ev_act(bi, t0, n_, buf, bkey, own):
                ACT(deint(buf, g, t0, n_, own), nat(pb[bi][:, 0:n_], g), AF.Copy, [PS(bi)], [bkey])

            def ev_cp(bi, n_, j):
                ACT(xs_l[j][:, 0:n_], pb[bi][:, 0:n_], AF.Copy, [PS(bi)], [("xs", j)])

            def r_hi(n_, j):
                ACT(hiB[j][:, 0:n_], xs_l[j][:, 0:n_], AF.Copy, [("xs", j)], [("hiB", j)])

            def r_lo(n_, j):
                S.op("dve", "tensor_tensor", dict(out=loB[j][:, 0:n_], in0=xs_l[j][:, 0:n_], in1=hiB[j][:, 0:n_], op=ALU.subtract),
                     [("xs", j), ("hiB", j)], [("loB", j)])

            def r_mm(n_, j):
                MM(pb[PB_SW][:, 0:n_], permB_bf, hiB[j][:, 0:n_], True, False, [("hiB", j), "cbf"], [PS(PB_SW)])
                MM(pb[PB_SW][:, 0:n_], permB_bf, loB[j][:, 0:n_], False, True, [("loB", j), "cbf"], [PS(PB_SW)])

            def r_t2(t0, n_, j):
                TT(t2_l[j][0:32, 0:n_], pb[PB_SW][0:32, 0:n_], rope[0:32, 1, t0:t0 + n_], ALU.mult, [PS(PB_SW), "rope"], [("t2", j)])

            def r_fin(t0, n_, buf, bkey, own, j):
                xs_, t2_ = xs_l[j], t2_l[j]
                TT(xs_[0:32, 0:n_], xs_[0:32, 0:n_], rope[0:32, 0, t0:t0 + n_], ALU.mult, [("xs", j), "rope"], [("xs", j)])
                TT(deint(buf[0:32], g, t0, n_, own), nat(xs_[0:32, 0:n_], g), nat(t2_[0:32, 0:n_], g), ALU.add,
                   [("xs", j), ("t2", j), bkey], [bkey])

            def evac_v(bi, t0, n_):
                ACT(deint(vTst, g, t0, n_, False), nat(pb[bi][:, 0:n_], g), AF.Copy, [PS(bi)], ["vTst"])

            def transposes(c0, cn):
                bank = pb[PB_SW][:].bitcast(BF16)
                for c in range(cn):
                    ch = c0 + c
                    TR(bank[:, c * 128:(c + 1) * 128], vTst[:, ch * 128:(ch + 1) * 128], ["vTst"], [PS(PB_SW)], inc=(c == cn - 1))
                CP(vtok[:, c0 * 128:(c0 + cn) * 128], bank[:, 0:cn * 128], [PS(PB_SW)], [vkey])

            def pads():
                if GPAD[g]:
                    MEMSET(kT[:, 0:d * GRL[g]].rearrange("p (r c) -> p r c", r=d)[:, :, 0:64], 0.0, [kkey])
                    MEMSET(vTst[:, 0:d * GRL[g]].rearrange("p (r c) -> p r c", r=d)[:, :, 0:64], 0.0, ["vTst"])

            def family(tag, slices, kind, buf, bkey, own, pending):
                th.append(partial(getw, tag))
                for i in range(0, len(slices), 2):
                    sl = slices[i:i + 2]
                    banks = [(2 * pairno[0]) % nproj, (2 * pairno[0] + 1) % nproj][:len(sl)]
                    pairno[0] += 1
                    for (k0, k1) in [(0, 4)] + [(k, k + 2) for k in range(4, 16, 2)]:
                        th.append(partial(proj_part, tag, sl, banks, k0, k1))
                        if pending:
                            th.append(pending.pop(0))
                    th.extend(pending)
                    pending = []
                    for j, ((t0, n_), bi) in enumerate(zip(sl, banks)):
                        if kind == "v":
                            th.append(partial(evac_v, bi, t0, n_))
                        else:
                            th.append(partial(ev_act, bi, t0, n_, buf, bkey, own))
                            th.append(partial(ev_cp, bi, n_, j))
                    if kind != "v":
                        for j, ((t0, n_), bi) in enumerate(zip(sl, banks)):
                            pending.append(partial(r_hi, n_, j))
                        for j, ((t0, n_), bi) in enumerate(zip(sl, banks)):
                            pending.append(partial(r_lo, n_, j))
                        for j, ((t0, n_), bi) in enumerate(zip(sl, banks)):
                            pending.append(partial(r_mm, n_, j))
                            pending.append(partial(r_t2, t0, n_, j))
                        for j, ((t0, n_), bi) in enumerate(zip(sl, banks)):
                            pending.append(partial(r_fin, t0, n_, buf, bkey, own, j))
                return pending

            th.append(pads)
            pend = family(("kb", g, hs), ksl, "k", kT, kkey, False, [])
            pend = family(("vb", g, hs), ksl, "v", None, None, False, pend)
            nch = d * GRL[g] // 128
            pend = family(("qb", g, hs), [(0, 512), (512, 512)], "q", qT, qkey, True, pend)
            for c0 in range(0, nch, 8):
                th.append(partial(transposes, c0, min(8, nch - c0)))
                if pend:
                    th.append(pend.pop(0))
                if pend:
                    th.append(pend.pop(0))
            th.extend(pend)
            if g == 2:
                def gbproj():
                    wslot, wkey = W(("gb", hs))
                    sl = [(0, 512), (512, 512)]
                    proj_fm(wslot, wkey, 16, hT, hTkeys, sl, [0, 1])
                    for (t0, n_), bi in zip(sl, [0, 1]):
                        dst = sgb[:, t0:t0 + 512]
                        ACT(dst, pb[bi][:], AF.Exp, [PS(bi)], ["sgb"], scale=-1.0)
                        ACT(dst, dst, AF.Ln, ["sgb", "one"], ["sgb"], bias=one_t[:, 0:1])
                        ACT(dst, dst, AF.Exp, ["sgb"], ["sgb"], scale=-1.0)
                        TT(dst, dst, pb[bi][:], ALU.mult, ["sgb", PS(bi)], ["sgb"])
                th.append(gbproj)
            return th

        pbufsB = [ab(10752 + 128 * i, 10752 + 128 * (i + 1)) for i in range(8)]

        def attnB(n):
            hs, g = iters[n]
            kT, vtok, qT = kTs[n % 2], vtoks[n % 2], qTs[n % 2]
            kkey, vkey, qkey = ("kT", n % 2), ("vtok", n % 2), ("qT", n % 2)
            LAG = 3
            iss = []
            cons = []
            P = {}

            def issue(idx, kcol, qcol, n_, mask):
                i = idx % 8
                bi = 4 + (idx % 2)
                pbuf, pkey = pbufsB[i], ("pbufB", i)
                MM(pb[bi][:, 0:n_], kT[:, kcol:kcol + 128], qT[:, qcol:qcol + n_], True, True, [kkey, qkey], [PS(bi)])
                ACT(pbuf[:, 0:n_], pb[bi][:, 0:n_], AF.Exp, [PS(bi)], [pkey], scale=SCALE)
                TT(pbuf[:, 0:n_], pbuf[:, 0:n_], mask, ALU.mult, [pkey, "cbf"], [pkey])
                P[idx] = (pbuf, pkey)

            def consume(ob, ocol, n_, parts):
                for j, (vch, idx, pc) in enumerate(parts):
                    pbuf, pkey = P[idx]
                    MM(pb[ob][:, ocol:ocol + n_], vtok[:, vch * 128:(vch + 1) * 128], pbuf[:, pc:pc + n_],
                       (j == 0), (j == len(parts) - 1), [vkey, pkey], [PS(ob)])
                for j, (vch, idx, pc) in enumerate(parts):
                    pbuf, pkey = P[idx]
                    MM(pb[ob][:, 256 + ocol:256 + ocol + n_], ones_bf, pbuf[:, pc:pc + n_],
                       (j == 0), (j == len(parts) - 1), ["cbf", pkey], [PS(ob)])

            def combine(qq, ob):
                for (acc, akey, c0) in ((U, "U", 0), (L, "L", 256)):
                    src = pb[ob][:, c0:c0 + 256]
                    if g == 0:
                        CP(acc[:, qq * 256:(qq + 1) * 256], src, [PS(ob)], [akey])
                    elif g == 1:
                        av = acc.rearrange("p (i r) -> p r i", r=4)[:, qq, :]
                        TT(av, src, av, ALU.add, [PS(ob), akey], [akey])
                    else:
                        av = acc.rearrange("p (i r) -> p r i", r=16)[:, 4 * qq:4 * qq + 4, :]
                        TT(av, src.rearrange("p (r i) -> p r i", r=4), av, ALU.add, [PS(ob), akey], [akey])

            def epilogue():
                RCP(L, L, ["L"], ["L"])
                TT(U, U, L, ALU.mult, ["U", "L"], ["U"])
                TT(ybT[:, hs, :], U, sgb, ALU.mult, ["U", "sgb"], [("ybT", hs)])

            idx = 0
            for qq in range(4):
                ob = 6 if qq % 2 == 0 else 3
                if g == 2:
                    for rr in range(4):
                        r = 4 * qq + rr
                        iss.append(partial(issue, idx, r * 128, r * 64, 64, mask2))
                        cons.append((idx, partial(consume, ob, rr * 64, 64, [(r, idx, 0)])))
                        idx += 1
                else:
                    if g == 0:
                        kbase, qbase, jf, nb, seqstart, vbase = 0, 0, 2 * qq, 2, (qq == 0), 0
                    else:
                        kbase, qbase, jf, nb, seqstart, vbase = qq * 384, qq * 256, 0, 2, True, qq * 3
                    prev = None
                    for j in range(jf, jf + nb + 1):
                        if j == jf:
                            iss.append(partial(issue, idx, kbase + 128 * j, qbase + 128 * j, 128,
                                               mask0 if seqstart else maskB[:, 128:256]))
                            up = 0
                        elif j == jf + nb:
                            iss.append(partial(issue, idx, kbase + 128 * j, qbase + 128 * (j - 1), 128, maskB[:, 0:128]))
                            up = None
                        else:
                            iss.append(partial(issue, idx, kbase + 128 * j, qbase + 128 * (j - 1), 256, maskB))
                            up = 128
                        if prev is not None:
                            cons.append((idx, partial(consume, ob, (j - 1 - jf) * 128, 128,
                                                      [(vbase + j - 1, prev[0], prev[1]), (vbase + j, idx, 0)])))
                        prev = (idx, up)
                        idx += 1
                cons.append((idx - 1, partial(combine, qq, ob)))
            if g == 2:
                cons.append((idx - 1, epilogue))
            th = []
            ci = 0
            for i, f in enumerate(iss):
                th.append(f)
                while ci < len(cons) and cons[ci][0] <= i - LAG:
                    th.append(cons[ci][1])
                    ci += 1
            while ci < len(cons):
                th.append(cons[ci][1])
                ci += 1
            return th

        S.alias(("kT", 0), [("kaT", 0)])
        S.alias(("vtok", 0), [("vatok", 0)])
        S.alias(("qT", 0), [("qaT", 0)])
        S.alias(("xs", 0), ["sq"])
        S.alias(("xs", 1), ["stdt"])
        S.alias(("t2", 0), ["t1"])
        S.alias(("t2", 1), ["t2"])
        S.alias(("hiB", 0), ["qn"])
        S.alias(("loB", 0), ["qn"])
        S.alias(("hiB", 1), ["hiA"])
        S.alias(("loB", 1), ["loA"])
        S.dma("sp", rope[0:32, 0, :], ropeB_d[0], "c_rope", writes=["rope"])
        S.dma("sp", rope[0:32, 1, :], ropeB_d[1], "c_rope", writes=["rope"])
        interleave(attnA(7), prepB(0, nproj=2, PB_SW=2))
        S.alias(("kT", 1), [("kaT", 1)])
        S.alias(("vtok", 1), [("vatok", 1)])
        S.alias(("qT", 1), [("qaT", 1)])
        for i in range(8):
            S.alias(("pbufB", i), [("pbuf", i // 2)])
        S.alias("U", ["rec", "tmp"])
        S.alias("L", [("sga", 1)])
        S.alias("sgb", [("sga", 0)])
        for n in range(12):
            interleave(prepB(n + 1) if n < 11 else [], attnB(n)) if n < 11 else [f() for f in attnB(n)]
        stop_here("B", ybT[:].rearrange("p k n -> p (k n)"), 4096)
        S.barrier()

        wbig = [wsl[:, 4 * j:4 * j + 4].rearrange("p s k c -> p (s k c)").rearrange("p (k n) -> p k n", k=16) for j in range(2)]
        wrope = rope[:].rearrange("p a n -> p (a n)").bitcast(BF16).rearrange("p (k n) -> p k n", k=16)

        def bigbuf(cb):
            if cb in (0, 3):
                return wrope, ["rope"]
            j = cb - 1
            return wbig[j], [("w", 4 * j + i) for i in range(4)]

        def big_issue(cb):
            wv, wk = bigbuf(cb)
            src = wout_d[:, cb * 512:(cb + 1) * 512].rearrange("(k p) c -> p k c", p=128)
            S.dma("pool", wv, src, "wbig%d" % cb, writes=wk)

        big_issue(0)
        mergedT = ab(0, 8192).rearrange("p (k t) -> p k t", k=16)
        mw = [[af(8192 + (s * 4 + i) * 512, 8192 + (s * 4 + i + 1) * 512) for i in range(4)] for s in range(2)]
        yakeys = [("yaT", h) for h in range(8)]
        ybkeys = [("ybT", h) for h in range(4)]
        for c in range(16):
            wza, kza = W(("za", c), 4)
            wzb, kzb = W(("zb", c), 4)
            wa, kwa = W(("wba", c), 4)
            wb_, kwb = W(("wbb", c), 4)
            for s in range(2):
                t0 = s * 512
                b0 = 4 * s
                proj_fm(wza, kza, 16, hT, hTkeys, [(t0, 512)], [b0])
                proj_fm(wzb, kzb, 16, hT, hTkeys, [(t0, 512)], [b0 + 1])
                proj_fm(wa, kwa, 8, yaT, yakeys, [(t0, 512)], [b0 + 2])
                proj_fm(wb_, kwb, 4, ybT, ybkeys, [(t0, 512)], [b0 + 3])
                sa, sbb, m1, m2 = mw[s]
                ACT(sa, pb[b0][:], AF.Sigmoid, [PS(b0), "mb"], [("sa", s)], bias=mb[:, c:c + 1])
                ACT(sbb, pb[b0 + 1][:], AF.Sigmoid, [PS(b0 + 1), "mb"], [("sb", s)], bias=mb[:, 16 + c:17 + c])
                TT(m1, pb[b0 + 2][:], sa, ALU.mult, [PS(b0 + 2), ("sa", s)], [("m1", s)])
                TT(m2, pb[b0 + 3][:], sbb, ALU.mult, [PS(b0 + 3), ("sb", s)], [("m2", s)])
                TT(mergedT[:, c, t0:t0 + 512], m1, m2, ALU.add, [("m1", s), ("m2", s)], [("mT", c)])

        stop_here("M", ab(0, 8192), 16384)
        S.dma("sp", gain[:], gf_d, "c_gain", writes=["gain"])
        xflat = hT[:].bitcast(F32).rearrange("p k n -> p (k n)")
        for t in range(8):
            S.dma("sp", xflat[:, t * 2048:(t + 1) * 2048], x[t * 128:(t + 1) * 128, :], "xr%d" % t,
                  writes=hk(2 * t) + hk(2 * t + 1))
        big_issue(1)
        big_issue(2)
        mkeys = [("mT", c) for c in range(16)]
        junk2 = ab(8192, 9216)
        ost = [af(9216, 11264), af(11264, 13312)]
        otoks = []

        def final_norm(t):
            xt = xflat[:, t * 2048:(t + 1) * 2048]
            xkeys = hk(2 * t) + hk(2 * t + 1)
            o = ost[t % 2]
            ACT(junk2, xt, AF.Square, xkeys + ["stat"], ["junk2", ("fs", t)], accum_out=stat[:, 48 + t:49 + t])
            ACT(stat[:, 56 + t:57 + t], stat[:, 48 + t:49 + t], AF.Ln, [("fs", t), "eps"], [("fd", t)],
                scale=1.0 / D, bias=eps_t[:, 0:1])
            ACT(stat[:, 56 + t:57 + t], stat[:, 56 + t:57 + t], AF.Exp, [("fd", t)], [("fd", t)], scale=-0.5)
            STT(o, xt, stat[:, 56 + t:57 + t], gain[:], xkeys + [("fd", t), "gain"], [("ost", t % 2)])
            otoks.append(S.dma("sp", y[t * 128:(t + 1) * 128, :], o, "out%d" % (t % 2), reads=[("ost", t % 2)]))

        cnt = 0
        for cb in range(4):
            wv, wk = bigbuf(cb)
            for t in range(8):
                bi = cnt % 8
                cnt += 1
                for k in range(16):
                    MM(pb[bi][:], mergedT[:, k, t * 128:(t + 1) * 128], wv[:, k, :], (k == 0), (k == 15),
                       mkeys + wk, [PS(bi)])
                xv = xflat[:, t * 2048 + cb * 512:t * 2048 + (cb + 1) * 512]
                xk = hk(2 * t + cb // 2)
                TT(xv, pb[bi][:], xv, ALU.add, [PS(bi)] + xk, xk)
                if cb == 3:
                    final_norm(t)
            if cb == 0:
                big_issue(3)
        S.wait_token("sp", otoks[-1])
        S.wait_token("sp", otoks[-2])
        print("[sched] ops per engine:", S.check())
        with nc.Block() as block:
            S.emit(block)
    return nc


def _rope_tables(tok):
    tok = np.asarray(tok)
    row = (tok // 64).astype(np.float32)
    col = (tok % 64).astype(np.float32)
    pos = tok.astype(np.float32)

    def angles(p, dim, theta):
        expo = np.arange(0, dim, 2, dtype=np.float32) / np.float32(dim)
        inv = (np.float32(1.0) / np.power(np.float32(theta), expo)).astype(np.float32)
        ang = (p[:, None] * inv[None, :]).astype(np.float32)
        return np.cos(ang).astype(np.float32), np.sin(ang).astype(np.float32)

    cr, sr = angles(row, 64, 10000.0)
    cc, sc = angles(col, 64, 10000.0)
    CA = np.concatenate([cr, cr, cc, cc], axis=1).T
    SA = np.concatenate([-sr, sr, -sc, sc], axis=1).T
    cb, sbb = angles(pos, 32, 500000.0)
    CB = np.concatenate([cb, cb], axis=1).T
    SB = np.concatenate([-sbb, sbb], axis=1).T
    return (np.ascontiguousarray(np.stack([CA, SA])).astype(np.float32),
            np.ascontiguousarray(np.stack([CB, SB])).astype(np.float32))


def _consts():
    cbf = np.zeros((128, 1088), np.float32)
    cbf[:, 0:128] = np.eye(128, dtype=np.float32)
    cbf[:, 128:256] = 1.0
    kk = np.arange(128)[:, None]
    qq = np.arange(256)[None, :]
    band = ((kk <= qq) & (qq <= kk + 128)).astype(np.float32)
    cbf[:, 384:640] = band
    cbf[:, 640:768] = band[:, 128:256] * (kk >= 64)
    q64 = np.arange(64)[None, :]
    cbf[:, 768:832] = (np.abs(kk - q64) <= 64).astype(np.float32)
    cf = np.zeros((128, 384), np.float32)
    for m in range(128):
        cf[m + 32 if (m // 32) % 2 == 0 else m - 32, m] = 1.0
    for m in range(32):
        cf[m + 16 if m < 16 else m - 16, 128 + m] = 1.0
    cf[:, 256:384] = 1.0 / 128.0
    cbf[:, 832:960] = cf[:, 0:128]
    cbf[:, 960:1088] = cf[:, 128:256]
    return cbf, cf


_NC_CACHE = {}


def kernel(x, norm_gain, w_in, q_norm_gain, k_norm_gain, merge_gate_bias,
           w_branch_a, w_branch_b, w_out, final_norm_gain, _debug=False):
    x = np.asarray(x, np.float32)
    key = bool(_debug)
    if key not in _NC_CACHE:
        _NC_CACHE[key] = build_nc(debug=key)
    nc = _NC_CACHE[key]
    cbf, cf = _consts()
    shared = {
        "w_in": np.ascontiguousarray(np.asarray(w_in, np.float32)[0]),
        "wba": np.ascontiguousarray(np.asarray(w_branch_a, np.float32)[0]),
        "wbb": np.ascontiguousarray(np.asarray(w_branch_b, np.float32)[0]),
        "wout": np.ascontiguousarray(np.asarray(w_out, np.float32)[0]),
        "gn": np.ascontiguousarray(np.broadcast_to(np.asarray(norm_gain, np.float32)[0][None, :], (128, D))),
        "gf": np.ascontiguousarray(np.broadcast_to(np.asarray(final_norm_gain, np.float32)[None, :], (128, D))),
        "qkg": np.ascontiguousarray(np.stack([np.asarray(q_norm_gain, np.float32)[0],
                                              np.asarray(k_norm_gain, np.float32)[0]], axis=1)),
        "mbias": np.ascontiguousarray(np.asarray(merge_gate_bias, np.float32)[0].reshape(2, 16, 128).transpose(2, 0, 1).reshape(128, 32)),
        "cbf": cbf, "cf": cf,
    }
    in_maps = []
    perms = []
    for c in range(8):
        b, half = c // 2, c % 2
        tok = np.arange(T) if half == 0 else (T - 1 - np.arange(T))
        perms.append(tok)
        ra, rb = _rope_tables(tok)
        m = dict(shared)
        m["x"] = np.ascontiguousarray(x[b][tok])
        m["ropeA"] = ra
        m["ropeB"] = rb
        in_maps.append(m)
    res = run_bass_kernel_spmd(nc, in_maps, core_ids=list(range(8)))
    out = np.empty((4, T, D), np.float32)
    for c in range(8):
        b = c // 2
        out[b, perms[c][:TO]] = np.asarray(res.results[c]["y"], np.float32)
    if _debug:
        return out, res
    return out
```

```python
from contextlib import ExitStack
from functools import partial
import numpy as np
import concourse.bass as bass
import concourse.mybir as mybir
from concourse.bass_utils import run_bass_kernel_spmd

F32 = mybir.dt.float32
BF16 = mybir.dt.bfloat16
AF = mybir.ActivationFunctionType
ALU = mybir.AluOpType

D = 2048
T = 2048
TO = 1024
IN_COLS = 11776
QA0, KA0, VA0, GA0, QB0, KB0, VB0, GB0, ZA0, ZB0 = 0, 1024, 1280, 1536, 2560, 4096, 5632, 7168, 7680, 9728
SCALE = float(128 ** -0.5)
EPS = 1e-6
NSLOT = 8
ENGS = ("pe", "act", "dve", "pool", "sp")


class Sched:
    def __init__(self, nc, es):
        self.nc = nc
        self.es = es
        self.sem = {e: es.enter_context(nc.semaphore("s_" + e)) for e in ENGS}
        self.cnt = {e: 0 for e in ENGS}
        self.prog = {e: [] for e in ENGS}
        self.waited = {}
        self.lastw = {}
        self.readers = {}
        self.dmasem = {}
        self.stopped = False

    def _deps(self, eng, reads, writes, is_dma):
        toks = []
        for k in reads:
            t = self.lastw.get(k)
            if t is not None:
                toks.append(t)
            if isinstance(k, tuple) and k[0] == "ps":
                for r in self.readers.get(k, ()):
                    if r[2] != eng:
                        toks.append(r)
        for k in writes:
            t = self.lastw.get(k)
            if t is not None and (is_dma or t[2] != eng or eng != "pe"):
                toks.append(t)
            for r in self.readers.get(k, ()):
                if is_dma or r[2] != eng or eng != "pe":
                    toks.append(r)
        best = {}
        for t in toks:
            if t[3] not in best or best[t[3]][1] < t[1]:
                best[t[3]] = t
        out = []
        for nm, t in best.items():
            key = (eng, nm)
            if self.waited.get(key, 0) >= t[1]:
                continue
            self.waited[key] = t[1]
            out.append((t[0], t[1]))
        return out

    def _commit(self, tok, reads, writes):
        for k in writes:
            self.lastw[k] = tok
            self.readers[k] = []
        for k in reads:
            self.readers.setdefault(k, []).append(tok)

    def op(self, eng, method, kw, reads=(), writes=(), inc=True):
        if self.stopped:
            return None
        fn = (method, kw)
        waits = self._deps(eng, reads, writes, False)
        if inc:
            self.cnt[eng] += 1
            tok = (self.sem[eng], self.cnt[eng], eng, "s_" + eng)
        else:
            tok = (self.sem[eng], self.cnt[eng] + 1, eng, "s_" + eng)
        self.prog[eng].append((fn, waits, (self.sem[eng], 1) if inc else None))
        self._commit(tok, reads, writes)
        return tok

    def dma(self, eng, out, in_, semname, reads=(), writes=()):
        if self.stopped:
            return None
        fn = ("dma_start", dict(out=out, in_=in_))
        if semname not in self.dmasem:
            self.dmasem[semname] = [self.es.enter_context(self.nc.semaphore("d_" + semname)), 0]
        ent = self.dmasem[semname]
        waits = self._deps(eng, reads, writes, True)
        ent[1] += 16
        tok = (ent[0], ent[1], None, "d_" + semname)
        self.prog[eng].append((fn, waits, (ent[0], 16)))
        self._commit(tok, reads, writes)
        return tok

    def alias(self, new_key, old_keys):
        toks = []
        for k in old_keys:
            if self.lastw.get(k) is not None:
                toks.append(self.lastw[k])
            toks.extend(self.readers.get(k, ()))
        self.readers.setdefault(new_key, []).extend(toks)

    def wait_token(self, eng, tok):
        if self.stopped or tok is None:
            return
        key = (eng, tok[3])
        if self.waited.get(key, 0) >= tok[1]:
            return
        self.waited[key] = tok[1]
        self.prog[eng].append((None, [(tok[0], tok[1])], None))

    def barrier(self):
        if self.stopped:
            return
        toks = []
        for e in ENGS:
            if self.cnt[e] > 0:
                toks.append((self.sem[e], self.cnt[e], e, "s_" + e))
        for nm, ent in self.dmasem.items():
            if ent[1] > 0:
                toks.append((ent[0], ent[1], None, "d_" + nm))
        for e in ENGS:
            for t in toks:
                if t[2] != e:
                    self.wait_token(e, t)

    def check(self):
        semv = {}
        pc = {e: 0 for e in ENGS}
        progress = True
        while progress:
            progress = False
            for e in ENGS:
                prog = self.prog[e]
                while pc[e] < len(prog):
                    fn, waits, inc = prog[pc[e]]
                    if any(semv.get(id(s_), 0) < v for (s_, v) in waits):
                        break
                    if inc is not None:
                        semv[id(inc[0])] = semv.get(id(inc[0]), 0) + inc[1]
                    pc[e] += 1
                    progress = True
        stuck = {e: (pc[e], len(self.prog[e])) for e in ENGS if pc[e] < len(self.prog[e])}
        if stuck:
            msg = []
            for e, (i, n) in stuck.items():
                fn, waits, inc = self.prog[e][i]
                msg.append("%s stuck at %d/%d %s waits=%s" % (e, i, n, fn[0] if fn else None,
                           [(str(s_), v, semv.get(id(s_), 0)) for (s_, v) in waits]))
            raise RuntimeError("DEADLOCK: " + " | ".join(msg))
        return {e: len(self.prog[e]) for e in ENGS}

    def emit(self, block):
        names = {"pe": "tensor", "act": "scalar", "dve": "vector", "pool": "gpsimd", "sp": "sync"}
        for e in ENGS:
            prog = self.prog[e]
            if not prog:
                continue

            def body(engine, prog=prog):
                for fn, waits, inc in prog:
                    for (s, v) in waits:
                        engine.wait_ge(s, v)
                    if fn is not None:
                        ins = getattr(engine, fn[0])(**fn[1])
                        if inc is not None:
                            ins.then_inc(inc[0], inc[1])

            getattr(block, names[e])(body)


def build_nc(debug=False, stop_after=None):
    nc = bass.Bass("TRN2", target_bir_lowering=False)

    def din(name, shape):
        return nc.dram_tensor(name, shape, F32, kind="ExternalInput").ap()

    x = din("x", [T, D])
    w_in = din("w_in", [D, IN_COLS])
    wba_d = din("wba", [1024, D])
    wbb_d = din("wbb", [512, D])
    wout_d = din("wout", [D, D])
    gn_d = din("gn", [128, D])
    gf_d = din("gf", [128, D])
    qkg_d = din("qkg", [128, 2])
    mb_d = din("mbias", [128, 32])
    ropeA_d = din("ropeA", [2, 128, T])
    ropeB_d = din("ropeB", [2, 32, T])
    cbf_d = din("cbf", [128, 1088])
    cf_d = din("cf", [128, 384])
    y = nc.dram_tensor("y", [TO, D], F32, kind="ExternalOutput").ap()
    dbg = None
    if debug:
        dbg = nc.dram_tensor("dbg", [128, 32768], BF16, kind="ExternalOutput").ap()

    es = ExitStack()
    with es:
        es.enter_context(nc.allow_low_precision("bf16 matmul operands, fp32 accumulation"))

        def sb(name, shape, dt):
            return es.enter_context(nc.sbuf_tensor(name, shape, dt))

        hT = sb("s_hT", [128, 16, 2048], BF16)
        wsl = sb("s_wsl", [128, NSLOT, 16, 128], BF16)
        yaT = sb("s_yaT", [128, 8, 1024], BF16)
        ybT = sb("s_ybT", [128, 4, 1024], BF16)
        rope = sb("s_rope", [128, 2, 2048], F32)
        gain = sb("s_gain", [128, 2048], F32)
        arena = sb("s_arena", [128, 13312], F32)
        cbf = sb("s_cbf", [128, 1088], BF16)
        cf = sb("s_cf", [128, 384], F32)
        qkg = sb("s_qkg", [128, 2], F32)
        mb = sb("s_mb", [128, 32], F32)
        eps_t = sb("s_eps_t", [128, 1], F32)
        one_t = sb("s_one_t", [128, 1], F32)
        stat = sb("s_stat", [128, 64], F32)
        pb = [es.enter_context(nc.psum_tensor("pb%d" % i, [128, 512], F32)) for i in range(8)]
        S = Sched(nc, es)

        ident = cbf[:, 0:128]
        ones_bf = cbf[:, 128:256]
        maskB = cbf[:, 384:640]
        mask0 = cbf[:, 640:768]
        mask2 = cbf[:, 768:832]
        permA_bf = cbf[:, 832:960]
        permB_bf = cbf[:, 960:1088]
        permA = cf[:, 0:128]
        permB = cf[:, 128:256]
        meanF = cf[:, 256:384]

        def af(a, b):
            return arena[:, a:b]

        def ab(a, b):
            return arena[:, a:b].bitcast(BF16)

        def PS(i):
            return ("ps", i)

        class _Stop(Exception):
            pass

        def stop_here(name, ap2d, ncols):
            if stop_after != name:
                return
            S.barrier()
            tk = S.dma("sp", dbg[:, 0:ncols], ap2d, "dbgout")
            S.wait_token("sp", tk)
            S.stopped = True

        def MM(out, lhsT, rhs, start, stop, reads, writes, inc=None):
            S.op("pe", "matmul", dict(out=out, lhsT=lhsT, rhs=rhs, start=start, stop=stop), reads, writes,
                 inc=(stop if inc is None else inc))

        def TR(out, in_, reads, writes, inc):
            S.op("pe", "transpose", dict(out=out, in_=in_, identity=ident), list(reads) + ["cbf"], writes, inc=inc)

        def ACT(out, in_, func, reads, writes, **kw):
            S.op("act", "activation", dict(out=out, in_=in_, func=func, **kw), reads, writes)

        def TT(out, in0, in1, op, reads, writes, eng="dve"):
            S.op(eng, "tensor_tensor", dict(out=out, in0=in0, in1=in1, op=op), reads, writes)

        def CP(out, in_, reads, writes, eng="dve"):
            S.op(eng, "tensor_copy", dict(out=out, in_=in_), reads, writes)

        def RCP(out, in_, reads, writes):
            S.op("dve", "reciprocal", dict(out=out, in_=in_), reads, writes)

        def STT(out, in0, scalar, in1, reads, writes):
            S.op("dve", "scalar_tensor_tensor", dict(out=out, in0=in0, scalar=scalar, in1=in1, op0=ALU.mult, op1=ALU.mult),
                 reads, writes)

        def MEMSET(ap, val, writes, eng="pool"):
            S.op(eng, "memset", dict(ap=ap, constant=val), (), writes)

        wlist = []

        def wsrc(dram, c0):
            return dram[:, c0:c0 + 128].rearrange("(k p) c -> p k c", p=128)

        for kvh in range(2):
            wlist.append((("ka", kvh), wsrc(w_in, KA0 + kvh * 128), 16))
            wlist.append((("va", kvh), wsrc(w_in, VA0 + kvh * 128), 16))
        for h in range(8):
            wlist.append((("qa", h), wsrc(w_in, QA0 + h * 128), 16))
            wlist.append((("ga", h), wsrc(w_in, GA0 + h * 128), 16))
        for hs in range(4):
            for g in range(3):
                c = g * 512 + hs * 128
                wlist.append((("kb", g, hs), wsrc(w_in, KB0 + c), 16))
                wlist.append((("vb", g, hs), wsrc(w_in, VB0 + c), 16))
                wlist.append((("qb", g, hs), wsrc(w_in, QB0 + c), 16))
            wlist.append((("gb", hs), wsrc(w_in, GB0 + hs * 128), 16))
        for c in range(16):
            wlist.append((("za", c), wsrc(w_in, ZA0 + c * 128), 16))
            wlist.append((("zb", c), wsrc(w_in, ZB0 + c * 128), 16))
            wlist.append((("wba", c), wsrc(wba_d, c * 128), 8))
            wlist.append((("wbb", c), wsrc(wbb_d, c * 128), 4))
        wstate = {"issued": 0, "used": 0}

        def w_issue():
            i = wstate["issued"]
            if i >= len(wlist):
                return
            tag, src, nk = wlist[i]
            slot = i % NSLOT
            S.dma("pool", wsl[:, slot, 0:nk, :], src, "w%d" % slot, writes=[("w", slot)])
            wstate["issued"] += 1

        def W(tag, lookahead=NSLOT - 1):
            i = wstate["used"]
            assert wlist[i][0] == tag, (wlist[i][0], tag)
            while wstate["issued"] < min(len(wlist), i + 1 + lookahead):
                w_issue()
            wstate["used"] += 1
            slot = i % NSLOT
            return wsl[:, slot], ("w", slot)

        S.dma("sp", gain[:], gn_d, "c_gain", writes=["gain"])
        S.dma("sp", cf[:], cf_d, "c_cf", writes=["cf"])
        S.dma("sp", qkg[:], qkg_d, "c_qkg", writes=["qkg"])
        S.dma("sp", mb[:], mb_d, "c_mb", writes=["mb"])
        S.dma("pool", cbf[:], cbf_d, "c_cbf", writes=["cbf"])
        MEMSET(eps_t[:], EPS, ["eps"])
        MEMSET(one_t[:], 1.0, ["one"])
        MEMSET(stat[:], 0.0, ["stat"])
        for i in range(NSLOT - 1):
            w_issue()
        S.dma("sp", rope[:, 0, :], ropeA_d[0], "c_rope", writes=["rope"])
        S.dma("sp", rope[:, 1, :], ropeA_d[1], "c_rope", writes=["rope"])

        xbs = [yaT[:, 4 * i:4 * i + 4, :].rearrange("p h n -> p (h n)").bitcast(F32) for i in range(2)]
        xbk = [[("yaT", h) for h in range(4 * i, 4 * i + 4)] for i in range(2)]
        xbs += [af(5120, 7168)]
        xbk += [["xb2"]]
        NXB = 3
        hbs = [ybT[:, 2 * i:2 * i + 2, :].rearrange("p h n -> p (h n)") for i in range(2)]
        hbk = [[("ybT", 2 * i), ("ybT", 2 * i + 1)] for i in range(2)]
        junk = ab(10752, 11776)
        jk = ["xb3"]

        def hk(k, t0=None, n=None):
            if t0 is None:
                return [("hT", k, s_) for s_ in range(4)]
            return [("hT", k, s_) for s_ in range(t0 // 512, (t0 + n - 1) // 512 + 1)]

        def p1load(t):
            p4 = t % NXB
            S.dma("sp", xbs[p4], x[t * 128:(t + 1) * 128, :], "x%d" % p4, writes=xbk[p4])

        def p1a(t):
            xb, hb, p2, p4 = xbs[t % NXB], hbs[t % 2], t % 2, t % NXB
            if t + 2 < 16:
                p1load(t + 2)
            ACT(junk, xb, AF.Square, xbk[p4] + ["stat"], jk + [("st", t)], accum_out=stat[:, t:t + 1])
            ACT(stat[:, 16 + t:17 + t], stat[:, t:t + 1], AF.Ln, [("st", t), "eps"], [("sd", t)],
                scale=1.0 / D, bias=eps_t[:, 0:1])
            ACT(stat[:, 32 + t:33 + t], stat[:, 16 + t:17 + t], AF.Exp, [("sd", t)], [("rs", t)], scale=-0.5)
            STT(hb, xb, stat[:, 32 + t:33 + t], gain[:], xbk[p4] + [("rs", t), "gain"], hbk[p2])

        def p1b(t):
            hb, p2 = hbs[t % 2], t % 2
            for half in range(2):
                bi = 6 + half
                bank = pb[bi][:].bitcast(BF16)
                for c in range(8):
                    k = half * 8 + c
                    TR(bank[:, c * 128:(c + 1) * 128], hb[:, k * 128:(k + 1) * 128], hbk[p2], [PS(bi)], inc=(c == 7))
                dst = hT[:, half * 8:(half + 1) * 8, t * 128:(t + 1) * 128]
                src = bank.rearrange("p (c n) -> p c n", c=8)
                wr = [("hT", half * 8 + c, t // 4) for c in range(8)]
                if half == 0:
                    ACT(dst, src, AF.Copy, [PS(bi)], wr)
                else:
                    CP(dst, src, [PS(bi)], wr)

        p1load(0)
        p1load(1)
        p1a(0)
        p1_grp = {1: [], 2: [], 3: []}
        for t in range(16):
            if t < 4:
                p1a(t + 1)
                p1b(t)
            else:
                grp = p1_grp[t // 4]
                if t + 1 < 16:
                    grp.append(partial(p1a, t + 1))
                grp.append(partial(p1b, t))

        def proj_fm(wslot, wkey, nk, src, srckeys, slices, banks):
            for k in range(nk):
                for (t0, n), bi in zip(slices, banks):
                    sk = hk(k, t0, n) if srckeys is None else srckeys
                    MM(pb[bi][:, 0:n], wslot[:, k, :], src[:, k, t0:t0 + n], (k == 0), (k == nk - 1),
                       [wkey] + sk, [PS(bi)])

        hTkeys = None

        kaTs = [ab(0, 1024), ab(1024, 2048)]
        vTst = ab(2048, 3072)
        vatoks = [ab(3072, 4096), ab(4096, 5120)]
        qaTs = [ab(5120, 5632), ab(5632, 6144)]
        sgas = [af(6144, 7168), af(7168, 8192)]
        sq = af(8192, 8704)
        stdt = af(8704, 9216)
        qn = af(9216, 9728)
        t1 = af(9728, 10240)
        t2 = af(10240, 10752)
        pbufs = [ab(10752 + 256 * i, 11008 + 256 * i) for i in range(4)]
        rec = af(11776, 12288)
        tmp = af(12288, 12800)

        sqb = sq.bitcast(BF16)[:, 0:512]
        hiA = ab(12800, 13056)
        loA = ab(13056, 13312)

        def qk_norm_rope_A(bi, gcol, t0, dst, dstkey):
            ps = pb[bi][:]
            ACT(sqb, ps, AF.Square, [PS(bi)], ["sq"])
            MM(pb[2][:], ones_bf, sqb, True, True, ["sq", "cbf"], [PS(2)])
            ACT(stdt, pb[2][:], AF.Ln, [PS(2), "eps"], ["stdt"], scale=1.0 / 128, bias=eps_t[:, 0:1])
            ACT(stdt, stdt, AF.Exp, ["stdt"], ["stdt"], scale=-0.5)
            STT(qn, ps, qkg[:, gcol:gcol + 1], stdt, [PS(bi), "qkg", "stdt"], ["qn"])
            MM(pb[3][:], permA, qn, True, True, ["qn", "cf"], [PS(3)])
            TT(t1, qn, rope[:, 0, t0:t0 + 512], ALU.mult, ["qn", "rope"], ["t1"])
            TT(t2, pb[3][:], rope[:, 1, t0:t0 + 512], ALU.mult, [PS(3), "rope"], ["t2"])
            TT(dst, t1, t2, ALU.add, ["t1", "t2"], [dstkey])

        def interleave(main, side):
            nm, ns = len(main), len(side)
            j = 0
            for i, f in enumerate(main):
                f()
                tgt = ((i + 1) * ns) // nm
                while j < min(tgt, ns):
                    side[j]()
                    j += 1
            while j < ns:
                side[j]()
                j += 1

        def nop():
            pass

        wkv = {}
        for kvh in range(2):
            wkv[("ka", kvh)] = W(("ka", kvh), 4)
            wkv[("va", kvh)] = W(("va", kvh), 4)

        def kv_proj(tag, sl, banks, k0, k1):
            wslot, wkey = wkv[tag]
            for k in range(k0, k1):
                for (t0, n), bi in zip(sl, banks):
                    MM(pb[bi][:, 0:n], wslot[:, k, :], hT[:, k, t0:t0 + n], (k == 0), (k == 15), [wkey] + hk(k, t0, n), [PS(bi)])

        def kv_proj1(tag, t0, bi, k0, k1):
            wslot, wkey = wkv[tag]
            for k in range(k0, k1):
                MM(pb[bi][:, 0:512], wslot[:, k, :], hT[:, k, t0:t0 + 512], (k == 0), (k == 15), [wkey] + hk(k, t0, 512), [PS(bi)])

        def k_units(sl_idx):
            th = []
            t0 = sl_idx * 512
            for kvh in range(2):
                kaT, kkey = kaTs[kvh], ("kaT", kvh)
                bi = kvh
                for k0 in range(0, 16, 4):
                    th.append(partial(kv_proj1, ("ka", kvh), t0, bi, k0, k0 + 4))

                def k_sq(bi=bi):
                    ACT(sqb, pb[bi][:], AF.Square, [PS(bi)], ["sq"])

                def k_ssb():
                    MM(pb[2][:], ones_bf, sqb, True, True, ["sq", "cbf"], [PS(2)])

                def k_sqrt():
                    ACT(stdt, pb[2][:], AF.Ln, [PS(2), "eps"], ["stdt"], scale=1.0 / 128, bias=eps_t[:, 0:1])

                def k_rcp():
                    ACT(stdt, stdt, AF.Exp, ["stdt"], ["stdt"], scale=-0.5)

                def k_stt(bi=bi):
                    STT(qn, pb[bi][:], qkg[:, 1:2], stdt, [PS(bi), "qkg", "stdt"], ["qn"])

                def k_hi():
                    CP(hiA, qn, ["qn"], ["hiA"])

                def k_lo():
                    S.op("dve", "tensor_tensor", dict(out=loA, in0=qn, in1=hiA, op=ALU.subtract), ["qn", "hiA"], ["loA"])

                def k_swap():
                    MM(pb[3][:], permA_bf, hiA, True, False, ["hiA", "cbf"], [PS(3)])
                    MM(pb[3][:], permA_bf, loA, False, True, ["loA", "cbf"], [PS(3)])
                    TT(t1, qn, rope[:, 0, t0:t0 + 512], ALU.mult, ["qn", "rope"], ["t1"])

                def k_t2():
                    TT(t2, pb[3][:], rope[:, 1, t0:t0 + 512], ALU.mult, [PS(3), "rope"], ["t2"])

                def k_add(kaT=kaT, kkey=kkey):
                    TT(kaT[:, t0:t0 + 512], t1, t2, ALU.add, ["t1", "t2"], [kkey])
                th += [k_sq, k_ssb, k_sqrt, k_rcp, k_stt, k_hi, k_lo, k_swap, k_t2, k_add]
            return th

        vTsts = [vTst, af(7168, 8192).bitcast(BF16)]

        def v_units(sl_idx):
            th = []
            t0 = sl_idx * 512
            for kvh in range(2):
                vst, vsk = vTsts[kvh], ("vTst", kvh)
                bi = 4 + kvh
                for k0 in range(0, 16, 4):
                    th.append(partial(kv_proj1, ("va", kvh), t0, bi, k0, k0 + 4))

                def v_ev(bi=bi, vst=vst, vsk=vsk):
                    ACT(vst[:, t0:t0 + 512], pb[bi][:], AF.Copy, [PS(bi)], [vsk])
                th.append(v_ev)
            return th

        def v_transposes():
            th = []
            for kvh in range(2):
                vatok, vkey, vst, vsk = vatoks[kvh], ("vatok", kvh), vTsts[kvh], ("vTst", kvh)
                for half in range(2):
                    def v_tr(half=half, vatok=vatok, vkey=vkey, vst=vst, vsk=vsk):
                        bank = pb[6 + half][:].bitcast(BF16)
                        for c in range(8):
                            ch = half * 8 + c
                            TR(bank[:, c * 128:(c + 1) * 128], vst[:, ch * 128:(ch + 1) * 128], [vsk], [PS(6 + half)], inc=(c == 7))
                        CP(vatok[:, half * 1024:(half + 1) * 1024], bank, [PS(6 + half)], [vkey])
                    th.append(v_tr)
            return th

        def merge2(a, b):
            out = []
            a, b = list(a), list(b)
            while a or b:
                if a:
                    out.append(a.pop(0))
                if b:
                    out.append(b.pop(0))
            return out

        for sidx in range(3):
            interleave(p1_grp[sidx + 1], merge2(k_units(sidx), v_units(sidx)))
        stop_here("p1", hT[:].rearrange("p k n -> p (k n)"), 32768)
        S.alias(("qaT", 0), ["xb2"])
        S.alias(("qaT", 1), ["xb2"])
        S.alias(("sga", 0), ["xb2"])
        for i in range(4):
            S.alias(("pbuf", i), ["xb3"])
        S.alias("rec", ["xb3"])
        S.alias("tmp", ["xb3"])
        PREP0 = []


        slA = [(0, 512), (512, 512)]

        def prepA(h):
            th = []
            qaT, sga = qaTs[h % 2], sgas[h % 2]
            qkey, gkey = ("qaT", h % 2), ("sga", h % 2)
            st = {}

            def getw(tag):
                st[tag] = W(tag, 3) if h == 0 else W(tag)

            def proj_part(tag, k0, k1):
                wslot, wkey = st[tag]
                for k in range(k0, k1):
                    for (t0, n), bi in zip(slA, [0, 1]):
                        MM(pb[bi][:, 0:n], wslot[:, k, :], hT[:, k, t0:t0 + n], (k == 0), (k == 15), [wkey] + hk(k, t0, n), [PS(bi)])

            def c_sq(bi):
                ACT(sqb, pb[bi][:], AF.Square, [PS(bi)], ["sq"])

            def c_ssb():
                MM(pb[2][:], ones_bf, sqb, True, True, ["sq", "cbf"], [PS(2)])

            def c_sqrt():
                ACT(stdt, pb[2][:], AF.Ln, [PS(2), "eps"], ["stdt"], scale=1.0 / 128, bias=eps_t[:, 0:1])

            def c_rcp():
                ACT(stdt, stdt, AF.Exp, ["stdt"], ["stdt"], scale=-0.5)

            def c_stt(bi):
                STT(qn, pb[bi][:], qkg[:, 0:1], stdt, [PS(bi), "qkg", "stdt"], ["qn"])

            def c_hi():
                CP(hiA, qn, ["qn"], ["hiA"])

            def c_lo():
                S.op("dve", "tensor_tensor", dict(out=loA, in0=qn, in1=hiA, op=ALU.subtract), ["qn", "hiA"], ["loA"])

            def c_swap(t0):
                MM(pb[2][:], permA_bf, hiA, True, False, ["hiA", "cbf"], [PS(2)])
                MM(pb[2][:], permA_bf, loA, False, True, ["loA", "cbf"], [PS(2)])
                TT(t1, qn, rope[:, 0, t0:t0 + 512], ALU.mult, ["qn", "rope"], ["t1"])

            def c_t2(t0):
                TT(t2, pb[2][:], rope[:, 1, t0:t0 + 512], ALU.mult, [PS(2), "rope"], ["t2"])

            def c_add(t0):
                TT(qaT[:, t0:t0 + 512], t1, t2, ALU.add, ["t1", "t2"], [qkey])

            def silu(t0, bi):
                dst = sga[:, t0:t0 + 512]
                ACT(dst, pb[bi][:], AF.Exp, [PS(bi)], [gkey], scale=-1.0)
                ACT(dst, dst, AF.Ln, [gkey, "one"], [gkey], bias=one_t[:, 0:1])
                ACT(dst, dst, AF.Exp, [gkey], [gkey], scale=-1.0)
                TT(dst, dst, pb[bi][:], ALU.mult, [gkey, PS(bi)], [gkey])

            th.append(partial(getw, ("qa", h)))
            for k0 in range(0, 16, 2):
                th.append(partial(proj_part, ("qa", h), k0, k0 + 2))
            for (t0, n), bi in zip(slA, [0, 1]):
                th += [partial(c_sq, bi), c_ssb, c_sqrt, nop, c_rcp, nop, nop, partial(c_stt, bi), nop,
                       c_hi, c_lo, nop, partial(c_swap, t0), nop, partial(c_t2, t0), nop, partial(c_add, t0)]
            th.append(partial(getw, ("ga", h)))
            for k0 in range(0, 16, 2):
                th.append(partial(proj_part, ("ga", h), k0, k0 + 2))
            th.append(nop)
            for (t0, n), bi in zip(slA, [0, 1]):
                th.append(partial(silu, t0, bi))
            return th

        def attnA(h):
            th = []
            kvh = h // 4
            kaT, vatok = kaTs[kvh], vatoks[kvh]
            kkey, vkey = ("kaT", kvh), ("vatok", kvh)
            qaT, sga = qaTs[h % 2], sgas[h % 2]
            qkey, gkey = ("qaT", h % 2), ("sga", h % 2)

            SB = [3, 4, 5]

            def s_mm(q0, kc):
                bi = SB[kc % 3]
                MM(pb[bi][:], kaT[:, kc * 128:(kc + 1) * 128], qaT[:, q0:q0 + 512], True, True, [kkey, qkey], [PS(bi)])

            def stepa(q0, kc):
                bi = SB[kc % 3]
                ACT(pbufs[kc % 4], pb[bi][:], AF.Exp, [PS(bi)], [("pbuf", kc % 4)], scale=SCALE)
                if kc + 2 < 16:
                    s_mm(q0, kc + 2)

            def stepb(q0, kc):
                pbuf, pkey = pbufs[kc % 4], ("pbuf", kc % 4)
                MM(pb[6][:], vatok[:, kc * 128:(kc + 1) * 128], pbuf, (kc == 0), (kc == 15), [vkey, pkey], [PS(6)])
                MM(pb[7][:], ones_bf, pbuf, (kc == 0), (kc == 15), ["cbf", pkey], [PS(7)])

            def epi(q0):
                ACT(rec, pb[7][:], AF.Ln, [PS(7)], ["rec"])
                CP(tmp, pb[6][:], [PS(6)], ["tmp"])
                ACT(rec, rec, AF.Exp, ["rec"], ["rec"], scale=-1.0)
                TT(tmp, tmp, rec, ALU.mult, ["tmp", "rec"], ["tmp"])
                TT(yaT[:, h, q0:q0 + 512], tmp, sga[:, q0:q0 + 512], ALU.mult, ["tmp", gkey], [("yaT", h)])

            def start(q0):
                s_mm(q0, 0)
                s_mm(q0, 1)

            for qb in range(2):
                q0 = qb * 512
                th.append(partial(start, q0))
                for kc in range(16):
                    th.append(partial(stepa, q0, kc))
                    if kc >= 1:
                        th.append(partial(stepb, q0, kc - 1))
                th.append(partial(stepb, q0, 15))
                th.append(partial(epi, q0))
            return th

        interleave(k_units(3) + prepA(0), v_units(3) + v_transposes())
        S.alias(("sga", 1), [("vTst", 1)])
        S.alias("vTst", [("vTst", 0)])
        for h in range(7):
            interleave(attnA(h), prepA(h + 1))
        kTs = [ab(0, 1024), ab(1024, 2048)]
        vTst = ab(2048, 3072)
        vtoks = [ab(3072, 4096), ab(4096, 5120)]
        qTs = [ab(5120, 5632), ab(5632, 6144)]
        sgb = af(6144, 7168)
        xs = af(8192, 8704)
        U = af(11776, 12800)
        L = af(7168, 8192)
        GD = (1, 4, 16)
        GRL = (1152, 384, 128)
        GPAD = (64, 64, 0)
        GKTOK = (1088, 1280, 2048)

        def slices_of(ntok):
            out = []
            t0 = 0
            while t0 < ntok:
                n = min(512, ntok - t0)
                out.append((t0, n))
                t0 += n
            return out

        def deint(buf, g, t0, n, own):
            d = GD[g]
            if own:
                rl, pad = TO // d, 0
            else:
                rl, pad = GRL[g], GPAD[g]
            if d == 1:
                return buf[:, pad + t0:pad + t0 + n]
            i0, ni = t0 // d, n // d
            return buf[:, 0:d * rl].rearrange("p (r c) -> p r c", r=d)[:, :, pad + i0:pad + i0 + ni]

        def nat(ap, g):
            d = GD[g]
            if d == 1:
                return ap
            return ap.rearrange("p (i r) -> p r i", r=d)

        def qk_rope_B(bi, g, t0, n, buf, bkey, own):
            ACT(deint(buf, g, t0, n, own), nat(pb[bi][:, 0:n], g), AF.Copy, [PS(bi)], [bkey])
            CP(xs[:, 0:n], pb[bi][:, 0:n], [PS(bi)], ["xs"])
            MM(pb[2][:, 0:n], permB, xs[:, 0:n], True, True, ["xs", "cf"], [PS(2)])
            TT(t1[0:32, 0:n], xs[0:32, 0:n], rope[0:32, 0, t0:t0 + n], ALU.mult, ["xs", "rope"], ["t1"])
            TT(t2[0:32, 0:n], pb[2][0:32, 0:n], rope[0:32, 1, t0:t0 + n], ALU.mult, [PS(2), "rope"], ["t2"])
            TT(deint(buf[0:32], g, t0, n, own), nat(t1[0:32, 0:n], g), nat(t2[0:32, 0:n], g), ALU.add,
               ["t1", "t2", bkey], [bkey])

        xs_l = [af(8192, 8704), af(8704, 9216)]
        t2_l = [af(9728, 10240), af(10240, 10752)]
        hiB = [ab(9216, 9472), ab(12800, 13056)]
        loB = [ab(9472, 9728), ab(13056, 13312)]
        PB_SW = 7
        iters = [(hs, g) for hs in range(4) for g in range(3)]

        def prepB(n, nproj=3, PB_SW=7):
            hs, g = iters[n]
            d = GD[g]
            kT, vtok, qT = kTs[n % 2], vtoks[n % 2], qTs[n % 2]
            kkey, vkey, qkey = ("kT", n % 2), ("vtok", n % 2), ("qT", n % 2)
            ksl = slices_of(GKTOK[g])
            th = []
            st = {}
            pairno = [0]

            def getw(tag):
                st[tag] = W(tag)

            def proj_part(tag, sl, banks, k0, k1):
                wslot, wkey = st[tag]
                if k0 == 0:
                    order = [(k, j) for j in range(len(sl)) for k in range(k0, k1)]
                else:
                    order = [(k, j) for k in range(k0, k1) for j in range(len(sl))]
                for k, j in order:
                    (t0, n_), bi = sl[j], banks[j]
                    MM(pb[bi][:, 0:n_], wslot[:, k, :], hT[:, k, t0:t0 + n_], (k == 0), (k == 15), [wkey] + hk(k, t0, n_), [PS(bi)])

            def ev_act(bi, t0, n_, buf, bkey, own):
                ACT(deint(buf, g, t0, n_, own), nat(pb[bi][:, 0:n_], g), AF.Copy, [PS(bi)], [bkey])

            def ev_cp(bi, n_, j):
                ACT(xs_l[j][:, 0:n_], pb[bi][:, 0:n_], AF.Copy, [PS(bi)], [("xs", j)])

            def r_hi(n_, j):
                ACT(hiB[j][:, 0:n_], xs_l[j][:, 0:n_], AF.Copy, [("xs", j)], [("hiB", j)])

            def r_lo(n_, j):
                S.op("dve", "tensor_tensor", dict(out=loB[j][:, 0:n_], in0=xs_l[j][:, 0:n_], in1=hiB[j][:, 0:n_], op=ALU.subtract),
                     [("xs", j), ("hiB", j)], [("loB", j)])

            def r_mm(n_, j):
                MM(pb[PB_SW][:, 0:n_], permB_bf, hiB[j][:, 0:n_], True, False, [("hiB", j), "cbf"], [PS(PB_SW)])
                MM(pb[PB_SW][:, 0:n_], permB_bf, loB[j][:, 0:n_], False, True, [("loB", j), "cbf"], [PS(PB_SW)])

            def r_t2(t0, n_, j):
                TT(t2_l[j][0:32, 0:n_], pb[PB_SW][0:32, 0:n_], rope[0:32, 1, t0:t0 + n_], ALU.mult, [PS(PB_SW), "rope"], [("t2", j)])

            def r_fin(t0, n_, buf, bkey, own, j):
                xs_, t2_ = xs_l[j], t2_l[j]
                TT(xs_[0:32, 0:n_], xs_[0:32, 0:n_], rope[0:32, 0, t0:t0 + n_], ALU.mult, [("xs", j), "rope"], [("xs", j)])
                TT(deint(buf[0:32], g, t0, n_, own), nat(xs_[0:32, 0:n_], g), nat(t2_[0:32, 0:n_], g), ALU.add,
                   [("xs", j), ("t2", j), bkey], [bkey])

            def evac_v(bi, t0, n_):
                ACT(deint(vTst, g, t0, n_, False), nat(pb[bi][:, 0:n_], g), AF.Copy, [PS(bi)], ["vTst"])

            def transposes(c0, cn):
                bank = pb[PB_SW][:].bitcast(BF16)
                for c in range(cn):
                    ch = c0 + c
                    TR(bank[:, c * 128:(c + 1) * 128], vTst[:, ch * 128:(ch + 1) * 128], ["vTst"], [PS(PB_SW)], inc=(c == cn - 1))
                CP(vtok[:, c0 * 128:(c0 + cn) * 128], bank[:, 0:cn * 128], [PS(PB_SW)], [vkey])

            def pads():
                if GPAD[g]:
                    MEMSET(kT[:, 0:d * GRL[g]].rearrange("p (r c) -> p r c", r=d)[:, :, 0:64], 0.0, [kkey])
                    MEMSET(vTst[:, 0:d * GRL[g]].rearrange("p (r c) -> p r c", r=d)[:, :, 0:64], 0.0, ["vTst"])

            def family(tag, slices, kind, buf, bkey, own, pending):
                th.append(partial(getw, tag))
                for i in range(0, len(slices), 2):
                    sl = slices[i:i + 2]
                    banks = [(2 * pairno[0]) % nproj, (2 * pairno[0] + 1) % nproj][:len(sl)]
                    pairno[0] += 1
                    for (k0, k1) in [(0, 4)] + [(k, k + 2) for k in range(4, 16, 2)]:
                        th.append(partial(proj_part, tag, sl, banks, k0, k1))
                        if pending:
                            th.append(pending.pop(0))
                    th.extend(pending)
                    pending = []
                    for j, ((t0, n_), bi) in enumerate(zip(sl, banks)):
                        if kind == "v":
                            th.append(partial(evac_v, bi, t0, n_))
                        else:
                            th.append(partial(ev_act, bi, t0, n_, buf, bkey, own))
                            th.append(partial(ev_cp, bi, n_, j))
                    if kind != "v":
                        for j, ((t0, n_), bi) in enumerate(zip(sl, banks)):
                            pending.append(partial(r_hi, n_, j))
                        for j, ((t0, n_), bi) in enumerate(zip(sl, banks)):
                            pending.append(partial(r_lo, n_, j))
                        for j, ((t0, n_), bi) in enumerate(zip(sl, banks)):
                            pending.append(partial(r_mm, n_, j))
                            pending.append(partial(r_t2, t0, n_, j))
                        for j, ((t0, n_), bi) in enumerate(zip(sl, banks)):
                            pending.append(partial(r_fin, t0, n_, buf, bkey, own, j))
                return pending

            th.append(pads)
            pend = family(("kb", g, hs), ksl, "k", kT, kkey, False, [])
            pend = family(("vb", g, hs), ksl, "v", None, None, False, pend)
            nch = d * GRL[g] // 128
            pend = family(("qb", g, hs), [(0, 512), (512, 512)], "q", qT, qkey, True, pend)
            for c0 in range(0, nch, 8):
                th.append(partial(transposes, c0, min(8, nch - c0)))
                if pend:
                    th.append(pend.pop(0))
                if pend:
                    th.append(pend.pop(0))
            th.extend(pend)
            if g == 2:
                def gbproj():
                    wslot, wkey = W(("gb", hs))
                    sl = [(0, 512), (512, 512)]
                    proj_fm(wslot, wkey, 16, hT, hTkeys, sl, [0, 1])
                    for (t0, n_), bi in zip(sl, [0, 1]):
                        dst = sgb[:, t0:t0 + 512]
                        ACT(dst, pb[bi][:], AF.Exp, [PS(bi)], ["sgb"], scale=-1.0)
                        ACT(dst, dst, AF.Ln, ["sgb", "one"], ["sgb"], bias=one_t[:, 0:1])
                        ACT(dst, dst, AF.Exp, ["sgb"], ["sgb"], scale=-1.0)
                        TT(dst, dst, pb[bi][:], ALU.mult, ["sgb", PS(bi)], ["sgb"])
                th.append(gbproj)
            return th

        pbufsB = [ab(10752 + 128 * i, 10752 + 128 * (i + 1)) for i in range(8)]

        def attnB(n):
            hs, g = iters[n]
            kT, vtok, qT = kTs[n % 2], vtoks[n % 2], qTs[n % 2]
            kkey, vkey, qkey = ("kT", n % 2), ("vtok", n % 2), ("qT", n % 2)
            LAG = 3
            iss = []
            cons = []
            P = {}

            def issue(idx, kcol, qcol, n_, mask):
                i = idx % 8
                bi = 4 + (idx % 2)
                pbuf, pkey = pbufsB[i], ("pbufB", i)
                MM(pb[bi][:, 0:n_], kT[:, kcol:kcol + 128], qT[:, qcol:qcol + n_], True, True, [kkey, qkey], [PS(bi)])
                ACT(pbuf[:, 0:n_], pb[bi][:, 0:n_], AF.Exp, [PS(bi)], [pkey], scale=SCALE)
                TT(pbuf[:, 0:n_], pbuf[:, 0:n_], mask, ALU.mult, [pkey, "cbf"], [pkey])
                P[idx] = (pbuf, pkey)

            def consume(ob, ocol, n_, parts):
                for j, (vch, idx, pc) in enumerate(parts):
                    pbuf, pkey = P[idx]
                    MM(pb[ob][:, ocol:ocol + n_], vtok[:, vch * 128:(vch + 1) * 128], pbuf[:, pc:pc + n_],
                       (j == 0), (j == len(parts) - 1), [vkey, pkey], [PS(ob)])
                for j, (vch, idx, pc) in enumerate(parts):
                    pbuf, pkey = P[idx]
                    MM(pb[ob][:, 256 + ocol:256 + ocol + n_], ones_bf, pbuf[:, pc:pc + n_],
                       (j == 0), (j == len(parts) - 1), ["cbf", pkey], [PS(ob)])

            def combine(qq, ob):
                for (acc, akey, c0) in ((U, "U", 0), (L, "L", 256)):
                    src = pb[ob][:, c0:c0 + 256]
                    if g == 0:
                        CP(acc[:, qq * 256:(qq + 1) * 256], src, [PS(ob)], [akey])
                    elif g == 1:
                        av = acc.rearrange("p (i r) -> p r i", r=4)[:, qq, :]
                        TT(av, src, av, ALU.add, [PS(ob), akey], [akey])
                    else:
                        av = acc.rearrange("p (i r) -> p r i", r=16)[:, 4 * qq:4 * qq + 4, :]
                        TT(av, src.rearrange("p (r i) -> p r i", r=4), av, ALU.add, [PS(ob), akey], [akey])

            def epilogue():
                RCP(L, L, ["L"], ["L"])
                TT(U, U, L, ALU.mult, ["U", "L"], ["U"])
                TT(ybT[:, hs, :], U, sgb, ALU.mult, ["U", "sgb"], [("ybT", hs)])

            idx = 0
            for qq in range(4):
                ob = 6 if qq % 2 == 0 else 3
                if g == 2:
                    for rr in range(4):
                        r = 4 * qq + rr
                        iss.append(partial(issue, idx, r * 128, r * 64, 64, mask2))
                        cons.append((idx, partial(consume, ob, rr * 64, 64, [(r, idx, 0)])))
                        idx += 1
                else:
                    if g == 0:
                        kbase, qbase, jf, nb, seqstart, vbase = 0, 0, 2 * qq, 2, (qq == 0), 0
                    else:
                        kbase, qbase, jf, nb, seqstart, vbase = qq * 384, qq * 256, 0, 2, True, qq * 3
                    prev = None
                    for j in range(jf, jf + nb + 1):
                        if j == jf:
                            iss.append(partial(issue, idx, kbase + 128 * j, qbase + 128 * j, 128,
                                               mask0 if seqstart else maskB[:, 128:256]))
                            up = 0
                        elif j == jf + nb:
                            iss.append(partial(issue, idx, kbase + 128 * j, qbase + 128 * (j - 1), 128, maskB[:, 0:128]))
                            up = None
                        else:
                            iss.append(partial(issue, idx, kbase + 128 * j, qbase + 128 * (j - 1), 256, maskB))
                            up = 128
                        if prev is not None:
                            cons.append((idx, partial(consume, ob, (j - 1 - jf) * 128, 128,
                                                      [(vbase + j - 1, prev[0], prev[1]), (vbase + j, idx, 0)])))
                        prev = (idx, up)
                        idx += 1
                cons.append((idx - 1, partial(combine, qq, ob)))
            if g == 2:
                cons.append((idx - 1, epilogue))
            th = []
            ci = 0
            for i, f in enumerate(iss):
                th.append(f)
                while ci < len(cons) and cons[ci][0] <= i - LAG:
                    th.append(cons[ci][1])
                    ci += 1
            while ci < len(cons):
                th.append(cons[ci][1])
                ci += 1
            return th

        S.alias(("kT", 0), [("kaT", 0)])
        S.alias(("vtok", 0), [("vatok", 0)])
        S.alias(("qT", 0), [("qaT", 0)])
        S.alias(("xs", 0), ["sq"])
        S.alias(("xs", 1), ["stdt"])
        S.alias(("t2", 0), ["t1"])
        S.alias(("t2", 1), ["t2"])
        S.alias(("hiB", 0), ["qn"])
        S.alias(("loB", 0), ["qn"])
        S.alias(("hiB", 1), ["hiA"])
        S.alias(("loB", 1), ["loA"])
        S.dma("sp", rope[0:32, 0, :], ropeB_d[0], "c_rope", writes=["rope"])
        S.dma("sp", rope[0:32, 1, :], ropeB_d[1], "c_rope", writes=["rope"])
        interleave(attnA(7), prepB(0, nproj=2, PB_SW=2))
        S.alias(("kT", 1), [("kaT", 1)])
        S.alias(("vtok", 1), [("vatok", 1)])
        S.alias(("qT", 1), [("qaT", 1)])
        for i in range(8):
            S.alias(("pbufB", i), [("pbuf", i // 2)])
        S.alias("U", ["rec", "tmp"])
        S.alias("L", [("sga", 1)])
        S.alias("sgb", [("sga", 0)])
        for n in range(12):
            interleave(prepB(n + 1) if n < 11 else [], attnB(n)) if n < 11 else [f() for f in attnB(n)]
        stop_here("B", ybT[:].rearrange("p k n -> p (k n)"), 4096)
        S.barrier()

        wbig = [wsl[:, 4 * j:4 * j + 4].rearrange("p s k c -> p (s k c)").rearrange("p (k n) -> p k n", k=16) for j in range(2)]
        wrope = rope[:].rearrange("p a n -> p (a n)").bitcast(BF16).rearrange("p (k n) -> p k n", k=16)

        def bigbuf(cb):
            if cb in (0, 3):
                return wrope, ["rope"]
            j = cb - 1
            return wbig[j], [("w", 4 * j + i) for i in range(4)]

        def big_issue(cb):
            wv, wk = bigbuf(cb)
            src = wout_d[:, cb * 512:(cb + 1) * 512].rearrange("(k p) c -> p k c", p=128)
            S.dma("pool", wv, src, "wbig%d" % cb, writes=wk)

        big_issue(0)
        mergedT = ab(0, 8192).rearrange("p (k t) -> p k t", k=16)
        mw = [[af(8192 + (s * 4 + i) * 512, 8192 + (s * 4 + i + 1) * 512) for i in range(4)] for s in range(2)]
        yakeys = [("yaT", h) for h in range(8)]
        ybkeys = [("ybT", h) for h in range(4)]
        for c in range(16):
            wza, kza = W(("za", c), 4)
            wzb, kzb = W(("zb", c), 4)
            wa, kwa = W(("wba", c), 4)
            wb_, kwb = W(("wbb", c), 4)
            for s in range(2):
                t0 = s * 512
                b0 = 4 * s
                proj_fm(wza, kza, 16, hT, hTkeys, [(t0, 512)], [b0])
                proj_fm(wzb, kzb, 16, hT, hTkeys, [(t0, 512)], [b0 + 1])
                proj_fm(wa, kwa, 8, yaT, yakeys, [(t0, 512)], [b0 + 2])
                proj_fm(wb_, kwb, 4, ybT, ybkeys, [(t0, 512)], [b0 + 3])
                sa, sbb, m1, m2 = mw[s]
                ACT(sa, pb[b0][:], AF.Sigmoid, [PS(b0), "mb"], [("sa", s)], bias=mb[:, c:c + 1])
                ACT(sbb, pb[b0 + 1][:], AF.Sigmoid, [PS(b0 + 1), "mb"], [("sb", s)], bias=mb[:, 16 + c:17 + c])
                TT(m1, pb[b0 + 2][:], sa, ALU.mult, [PS(b0 + 2), ("sa", s)], [("m1", s)])
                TT(m2, pb[b0 + 3][:], sbb, ALU.mult, [PS(b0 + 3), ("sb", s)], [("m2", s)])
                TT(mergedT[:, c, t0:t0 + 512], m1, m2, ALU.add, [("m1", s), ("m2", s)], [("mT", c)])

        stop_here("M", ab(0, 8192), 16384)
        S.dma("sp", gain[:], gf_d, "c_gain", writes=["gain"])
        xflat = hT[:].bitcast(F32).rearrange("p k n -> p (k n)")
        for t in range(8):
            S.dma("sp", xflat[:, t * 2048:(t + 1) * 2048], x[t * 128:(t + 1) * 128, :], "xr%d" % t,
                  writes=hk(2 * t) + hk(2 * t + 1))
        big_issue(1)
        big_issue(2)
        mkeys = [("mT", c) for c in range(16)]
        junk2 = ab(8192, 9216)
        ost = [af(9216, 11264), af(11264, 13312)]
        otoks = []

        def final_norm(t):
            xt = xflat[:, t * 2048:(t + 1) * 2048]
            xkeys = hk(2 * t) + hk(2 * t + 1)
            o = ost[t % 2]
            ACT(junk2, xt, AF.Square, xkeys + ["stat"], ["junk2", ("fs", t)], accum_out=stat[:, 48 + t:49 + t])
            ACT(stat[:, 56 + t:57 + t], stat[:, 48 + t:49 + t], AF.Ln, [("fs", t), "eps"], [("fd", t)],
                scale=1.0 / D, bias=eps_t[:, 0:1])
            ACT(stat[:, 56 + t:57 + t], stat[:, 56 + t:57 + t], AF.Exp, [("fd", t)], [("fd", t)], scale=-0.5)
            STT(o, xt, stat[:, 56 + t:57 + t], gain[:], xkeys + [("fd", t), "gain"], [("ost", t % 2)])
            otoks.append(S.dma("sp" if t % 2 == 0 else "act", y[t * 128:(t + 1) * 128, :], o, "out%d" % (t % 2),
                               reads=[("ost", t % 2)]))

        cnt = 0
        for cb in range(4):
            wv, wk = bigbuf(cb)
            for t in range(8):
                bi = cnt % 8
                cnt += 1
                for k in range(16):
                    MM(pb[bi][:], mergedT[:, k, t * 128:(t + 1) * 128], wv[:, k, :], (k == 0), (k == 15),
                       mkeys + wk, [PS(bi)])
                xv = xflat[:, t * 2048 + cb * 512:t * 2048 + (cb + 1) * 512]
                xk = hk(2 * t + cb // 2)
                TT(xv, pb[bi][:], xv, ALU.add, [PS(bi)] + xk, xk)
                if cb == 3:
                    final_norm(t)
            if cb == 0:
                big_issue(3)
        S.wait_token("sp", otoks[-1])
        S.wait_token("sp", otoks[-2])
        print("[sched] ops per engine:", S.check())
        with nc.Block() as block:
            S.emit(block)
    return nc


def _rope_tables(tok):
    tok = np.asarray(tok)
    row = (tok // 64).astype(np.float32)
    col = (tok % 64).astype(np.float32)
    pos = tok.astype(np.float32)

    def angles(p, dim, theta):
        expo = np.arange(0, dim, 2, dtype=np.float32) / np.float32(dim)
        inv = (np.float32(1.0) / np.power(np.float32(theta), expo)).astype(np.float32)
        ang = (p[:, None] * inv[None, :]).astype(np.float32)
        return np.cos(ang).astype(np.float32), np.sin(ang).astype(np.float32)

    cr, sr = angles(row, 64, 10000.0)
    cc, sc = angles(col, 64, 10000.0)
    CA = np.concatenate([cr, cr, cc, cc], axis=1).T
    SA = np.concatenate([-sr, sr, -sc, sc], axis=1).T
    cb, sbb = angles(pos, 32, 500000.0)
    CB = np.concatenate([cb, cb], axis=1).T
    SB = np.concatenate([-sbb, sbb], axis=1).T
    return (np.ascontiguousarray(np.stack([CA, SA])).astype(np.float32),
            np.ascontiguousarray(np.stack([CB, SB])).astype(np.float32))


def _consts():
    cbf = np.zeros((128, 1088), np.float32)
    cbf[:, 0:128] = np.eye(128, dtype=np.float32)
    cbf[:, 128:256] = 1.0
    kk = np.arange(128)[:, None]
    qq = np.arange(256)[None, :]
    band = ((kk <= qq) & (qq <= kk + 128)).astype(np.float32)
    cbf[:, 384:640] = band
    cbf[:, 640:768] = band[:, 128:256] * (kk >= 64)
    q64 = np.arange(64)[None, :]
    cbf[:, 768:832] = (np.abs(kk - q64) <= 64).astype(np.float32)
    cf = np.zeros((128, 384), np.float32)
    for m in range(128):
        cf[m + 32 if (m // 32) % 2 == 0 else m - 32, m] = 1.0
    for m in range(32):
        cf[m + 16 if m < 16 else m - 16, 128 + m] = 1.0
    cf[:, 256:384] = 1.0 / 128.0
    cbf[:, 832:960] = cf[:, 0:128]
    cbf[:, 960:1088] = cf[:, 128:256]
    return cbf, cf


_NC_CACHE = {}


def kernel(x, norm_gain, w_in, q_norm_gain, k_norm_gain, merge_gate_bias,
           w_branch_a, w_branch_b, w_out, final_norm_gain, _debug=False):
    x = np.asarray(x, np.float32)
    key = bool(_debug)
    if key not in _NC_CACHE:
        _NC_CACHE[key] = build_nc(debug=key)
    nc = _NC_CACHE[key]
    cbf, cf = _consts()
    shared = {
        "w_in": np.ascontiguousarray(np.asarray(w_in, np.float32)[0]),
        "wba": np.ascontiguousarray(np.asarray(w_branch_a, np.float32)[0]),
        "wbb": np.ascontiguousarray(np.asarray(w_branch_b, np.float32)[0]),
        "wout": np.ascontiguousarray(np.asarray(w_out, np.float32)[0]),
        "gn": np.ascontiguousarray(np.broadcast_to(np.asarray(norm_gain, np.float32)[0][None, :], (128, D))),
        "gf": np.ascontiguousarray(np.broadcast_to(np.asarray(final_norm_gain, np.float32)[None, :], (128, D))),
        "qkg": np.ascontiguousarray(np.stack([np.asarray(q_norm_gain, np.float32)[0],
                                              np.asarray(k_norm_gain, np.float32)[0]], axis=1)),
        "mbias": np.ascontiguousarray(np.asarray(merge_gate_bias, np.float32)[0].reshape(2, 16, 128).transpose(2, 0, 1).reshape(128, 32)),
        "cbf": cbf, "cf": cf,
    }
    in_maps = []
    perms = []
    for c in range(8):
        b, half = c // 2, c % 2
        tok = np.arange(T) if half == 0 else (T - 1 - np.arange(T))
        perms.append(tok)
        ra, rb = _rope_tables(tok)
        m = dict(shared)
        m["x"] = np.ascontiguousarray(x[b][tok])
        m["ropeA"] = ra
        m["ropeB"] = rb
        in_maps.append(m)
    res = run_bass_kernel_spmd(nc, in_maps, core_ids=list(range(8)))
    out = np.empty((4, T, D), np.float32)
    for c in range(8):
        b = c // 2
        out[b, perms[c][:TO]] = np.asarray(res.results[c]["y"], np.float32)
    if _debug:
        return out, res
    return out
```

```python
from contextlib import ExitStack
from functools import partial
import numpy as np
import concourse.bass as bass
import concourse.mybir as mybir
from concourse.bass_utils import run_bass_kernel_spmd

F32 = mybir.dt.float32
BF16 = mybir.dt.bfloat16
AF = mybir.ActivationFunctionType
ALU = mybir.AluOpType

D = 2048
T = 2048
TO = 1024
IN_COLS = 11776
QA0, KA0, VA0, GA0, QB0, KB0, VB0, GB0, ZA0, ZB0 = 0, 1024, 1280, 1536, 2560, 4096, 5632, 7168, 7680, 9728
SCALE = float(128 ** -0.5)
EPS = 1e-6
NSLOT = 8
ENGS = ("pe", "act", "dve", "pool", "sp")


class Sched:
    def __init__(self, nc, es):
        self.nc = nc
        self.es = es
        self.sem = {e: es.enter_context(nc.semaphore("s_" + e)) for e in ENGS}
        self.cnt = {e: 0 for e in ENGS}
        self.prog = {e: [] for e in ENGS}
        self.waited = {}
        self.lastw = {}
        self.readers = {}
        self.dmasem = {}
        self.stopped = False

    def _deps(self, eng, reads, writes, is_dma):
        toks = []
        for k in reads:
            t = self.lastw.get(k)
            if t is not None:
                toks.append(t)
            if isinstance(k, tuple) and k[0] == "ps":
                for r in self.readers.get(k, ()):
                    if r[2] != eng:
                        toks.append(r)
        for k in writes:
            t = self.lastw.get(k)
            if t is not None and (is_dma or t[2] != eng or eng != "pe"):
                toks.append(t)
            for r in self.readers.get(k, ()):
                if is_dma or r[2] != eng or eng != "pe":
                    toks.append(r)
        best = {}
        for t in toks:
            if t[3] not in best or best[t[3]][1] < t[1]:
                best[t[3]] = t
        out = []
        for nm, t in best.items():
            key = (eng, nm)
            if self.waited.get(key, 0) >= t[1]:
                continue
            self.waited[key] = t[1]
            out.append((t[0], t[1]))
        return out

    def _commit(self, tok, reads, writes):
        for k in writes:
            self.lastw[k] = tok
            self.readers[k] = []
        for k in reads:
            self.readers.setdefault(k, []).append(tok)

    def op(self, eng, method, kw, reads=(), writes=(), inc=True):
        if self.stopped:
            return None
        fn = (method, kw)
        waits = self._deps(eng, reads, writes, False)
        if inc:
            self.cnt[eng] += 1
            tok = (self.sem[eng], self.cnt[eng], eng, "s_" + eng)
        else:
            tok = (self.sem[eng], self.cnt[eng] + 1, eng, "s_" + eng)
        self.prog[eng].append((fn, waits, (self.sem[eng], 1) if inc else None))
        self._commit(tok, reads, writes)
        return tok

    def dma(self, eng, out, in_, semname, reads=(), writes=()):
        if self.stopped:
            return None
        fn = ("dma_start", dict(out=out, in_=in_))
        if semname not in self.dmasem:
            self.dmasem[semname] = [self.es.enter_context(self.nc.semaphore("d_" + semname)), 0]
        ent = self.dmasem[semname]
        waits = self._deps(eng, reads, writes, True)
        ent[1] += 16
        tok = (ent[0], ent[1], None, "d_" + semname)
        self.prog[eng].append((fn, waits, (ent[0], 16)))
        self._commit(tok, reads, writes)
        return tok

    def alias(self, new_key, old_keys):
        toks = []
        for k in old_keys:
            if self.lastw.get(k) is not None:
                toks.append(self.lastw[k])
            toks.extend(self.readers.get(k, ()))
        self.readers.setdefault(new_key, []).extend(toks)

    def wait_token(self, eng, tok):
        if self.stopped or tok is None:
            return
        key = (eng, tok[3])
        if self.waited.get(key, 0) >= tok[1]:
            return
        self.waited[key] = tok[1]
        self.prog[eng].append((None, [(tok[0], tok[1])], None))

    def barrier(self):
        if self.stopped:
            return
        toks = []
        for e in ENGS:
            if self.cnt[e] > 0:
                toks.append((self.sem[e], self.cnt[e], e, "s_" + e))
        for nm, ent in self.dmasem.items():
            if ent[1] > 0:
                toks.append((ent[0], ent[1], None, "d_" + nm))
        for e in ENGS:
            for t in toks:
                if t[2] != e:
                    self.wait_token(e, t)

    def check(self):
        semv = {}
        pc = {e: 0 for e in ENGS}
        progress = True
        while progress:
            progress = False
            for e in ENGS:
                prog = self.prog[e]
                while pc[e] < len(prog):
                    fn, waits, inc = prog[pc[e]]
                    if any(semv.get(id(s_), 0) < v for (s_, v) in waits):
                        break
                    if inc is not None:
                        semv[id(inc[0])] = semv.get(id(inc[0]), 0) + inc[1]
                    pc[e] += 1
                    progress = True
        stuck = {e: (pc[e], len(self.prog[e])) for e in ENGS if pc[e] < len(self.prog[e])}
        if stuck:
            msg = []
            for e, (i, n) in stuck.items():
                fn, waits, inc = self.prog[e][i]
                msg.append("%s stuck at %d/%d %s waits=%s" % (e, i, n, fn[0] if fn else None,
                           [(str(s_), v, semv.get(id(s_), 0)) for (s_, v) in waits]))
            raise RuntimeError("DEADLOCK: " + " | ".join(msg))
        return {e: len(self.prog[e]) for e in ENGS}

    def emit(self, block):
        names = {"pe": "tensor", "act": "scalar", "dve": "vector", "pool": "gpsimd", "sp": "sync"}
        for e in ENGS:
            prog = self.prog[e]
            if not prog:
                continue

            def body(engine, prog=prog):
                for fn, waits, inc in prog:
                    for (s, v) in waits:
                        engine.wait_ge(s, v)
                    if fn is not None:
                        ins = getattr(engine, fn[0])(**fn[1])
                        if inc is not None:
                            ins.then_inc(inc[0], inc[1])

            getattr(block, names[e])(body)


def build_nc(debug=False, stop_after=None):
    nc = bass.Bass("TRN2", target_bir_lowering=False)

    def din(name, shape):
        return nc.dram_tensor(name, shape, F32, kind="ExternalInput").ap()

    x = din("x", [T, D])
    w_in = din("w_in", [D, IN_COLS])
    wba_d = din("wba", [1024, D])
    wbb_d = din("wbb", [512, D])
    wout_d = din("wout", [D, D])
    gn_d = din("gn", [128, D])
    gf_d = din("gf", [128, D])
    qkg_d = din("qkg", [128, 2])
    mb_d = din("mbias", [128, 32])
    ropeA_d = din("ropeA", [2, 128, T])
    ropeB_d = din("ropeB", [2, 32, T])
    cbf_d = din("cbf", [128, 1088])
    cf_d = din("cf", [128, 384])
    y = nc.dram_tensor("y", [TO, D], F32, kind="ExternalOutput").ap()
    dbg = None
    if debug:
        dbg = nc.dram_tensor("dbg", [128, 32768], BF16, kind="ExternalOutput").ap()

    es = ExitStack()
    with es:
        es.enter_context(nc.allow_low_precision("bf16 matmul operands, fp32 accumulation"))

        def sb(name, shape, dt):
            return es.enter_context(nc.sbuf_tensor(name, shape, dt))

        hT = sb("s_hT", [128, 16, 2048], BF16)
        wsl = sb("s_wsl", [128, NSLOT, 16, 128], BF16)
        yaT = sb("s_yaT", [128, 8, 1024], BF16)
        ybT = sb("s_ybT", [128, 4, 1024], BF16)
        rope = sb("s_rope", [128, 2, 2048], F32)
        gain = sb("s_gain", [128, 2048], F32)
        arena = sb("s_arena", [128, 13312], F32)
        cbf = sb("s_cbf", [128, 1088], BF16)
        cf = sb("s_cf", [128, 384], F32)
        qkg = sb("s_qkg", [128, 2], F32)
        mb = sb("s_mb", [128, 32], F32)
        eps_t = sb("s_eps_t", [128, 1], F32)
        one_t = sb("s_one_t", [128, 1], F32)
        stat = sb("s_stat", [128, 64], F32)
        pb = [es.enter_context(nc.psum_tensor("pb%d" % i, [128, 512], F32)) for i in range(8)]
        S = Sched(nc, es)

        ident = cbf[:, 0:128]
        ones_bf = cbf[:, 128:256]
        maskB = cbf[:, 384:640]
        mask0 = cbf[:, 640:768]
        mask2 = cbf[:, 768:832]
        permA_bf = cbf[:, 832:960]
        permB_bf = cbf[:, 960:1088]
        permA = cf[:, 0:128]
        permB = cf[:, 128:256]
        meanF = cf[:, 256:384]

        def af(a, b):
            return arena[:, a:b]

        def ab(a, b):
            return arena[:, a:b].bitcast(BF16)

        def PS(i):
            return ("ps", i)

        class _Stop(Exception):
            pass

        def stop_here(name, ap2d, ncols):
            if stop_after != name:
                return
            S.barrier()
            tk = S.dma("sp", dbg[:, 0:ncols], ap2d, "dbgout")
            S.wait_token("sp", tk)
            S.stopped = True

        def MM(out, lhsT, rhs, start, stop, reads, writes, inc=None):
            S.op("pe", "matmul", dict(out=out, lhsT=lhsT, rhs=rhs, start=start, stop=stop), reads, writes,
                 inc=(stop if inc is None else inc))

        def TR(out, in_, reads, writes, inc):
            S.op("pe", "transpose", dict(out=out, in_=in_, identity=ident), list(reads) + ["cbf"], writes, inc=inc)

        def ACT(out, in_, func, reads, writes, **kw):
            S.op("act", "activation", dict(out=out, in_=in_, func=func, **kw), reads, writes)

        def TT(out, in0, in1, op, reads, writes, eng="dve"):
            S.op(eng, "tensor_tensor", dict(out=out, in0=in0, in1=in1, op=op), reads, writes)

        def CP(out, in_, reads, writes, eng="dve"):
            S.op(eng, "tensor_copy", dict(out=out, in_=in_), reads, writes)

        def RCP(out, in_, reads, writes):
            S.op("dve", "reciprocal", dict(out=out, in_=in_), reads, writes)

        def STT(out, in0, scalar, in1, reads, writes):
            S.op("dve", "scalar_tensor_tensor", dict(out=out, in0=in0, scalar=scalar, in1=in1, op0=ALU.mult, op1=ALU.mult),
                 reads, writes)

        def MEMSET(ap, val, writes, eng="pool"):
            S.op(eng, "memset", dict(ap=ap, constant=val), (), writes)

        wlist = []

        def wsrc(dram, c0):
            return dram[:, c0:c0 + 128].rearrange("(k p) c -> p k c", p=128)

        for kvh in range(2):
            wlist.append((("ka", kvh), wsrc(w_in, KA0 + kvh * 128), 16))
            wlist.append((("va", kvh), wsrc(w_in, VA0 + kvh * 128), 16))
        for h in range(8):
            wlist.append((("qa", h), wsrc(w_in, QA0 + h * 128), 16))
            wlist.append((("ga", h), wsrc(w_in, GA0 + h * 128), 16))
        for hs in range(4):
            for g in range(3):
                c = g * 512 + hs * 128
                wlist.append((("kb", g, hs), wsrc(w_in, KB0 + c), 16))
                wlist.append((("vb", g, hs), wsrc(w_in, VB0 + c), 16))
                wlist.append((("qb", g, hs), wsrc(w_in, QB0 + c), 16))
            wlist.append((("gb", hs), wsrc(w_in, GB0 + hs * 128), 16))
        for c in range(16):
            wlist.append((("za", c), wsrc(w_in, ZA0 + c * 128), 16))
            wlist.append((("zb", c), wsrc(w_in, ZB0 + c * 128), 16))
            wlist.append((("wba", c), wsrc(wba_d, c * 128), 8))
            wlist.append((("wbb", c), wsrc(wbb_d, c * 128), 4))
        wstate = {"issued": 0, "used": 0}

        def w_issue():
            i = wstate["issued"]
            if i >= len(wlist):
                return
            tag, src, nk = wlist[i]
            slot = i % NSLOT
            S.dma("pool", wsl[:, slot, 0:nk, :], src, "w%d" % slot, writes=[("w", slot)])
            wstate["issued"] += 1

        def W(tag, lookahead=NSLOT - 1):
            i = wstate["used"]
            assert wlist[i][0] == tag, (wlist[i][0], tag)
            while wstate["issued"] < min(len(wlist), i + 1 + lookahead):
                w_issue()
            wstate["used"] += 1
            slot = i % NSLOT
            return wsl[:, slot], ("w", slot)

        S.dma("sp", gain[:], gn_d, "c_gain", writes=["gain"])
        S.dma("sp", cf[:], cf_d, "c_cf", writes=["cf"])
        S.dma("sp", qkg[:], qkg_d, "c_qkg", writes=["qkg"])
        S.dma("sp", mb[:], mb_d, "c_mb", writes=["mb"])
        S.dma("pool", cbf[:], cbf_d, "c_cbf", writes=["cbf"])
        MEMSET(eps_t[:], EPS, ["eps"])
        MEMSET(one_t[:], 1.0, ["one"])
        MEMSET(stat[:], 0.0, ["stat"])
        for i in range(NSLOT - 1):
            w_issue()
        S.dma("sp", rope[:, 0, :], ropeA_d[0], "c_rope", writes=["rope"])
        S.dma("sp", rope[:, 1, :], ropeA_d[1], "c_rope", writes=["rope"])

        xbs = [yaT[:, 4 * i:4 * i + 4, :].rearrange("p h n -> p (h n)").bitcast(F32) for i in range(2)]
        xbk = [[("yaT", h) for h in range(4 * i, 4 * i + 4)] for i in range(2)]
        xbs += [af(5120, 7168)]
        xbk += [["xb2"]]
        NXB = 3
        hbs = [ybT[:, 2 * i:2 * i + 2, :].rearrange("p h n -> p (h n)") for i in range(2)]
        hbk = [[("ybT", 2 * i), ("ybT", 2 * i + 1)] for i in range(2)]
        junk = ab(10752, 11776)
        jk = ["xb3"]

        def hk(k, t0=None, n=None):
            if t0 is None:
                return [("hT", k, s_) for s_ in range(4)]
            return [("hT", k, s_) for s_ in range(t0 // 512, (t0 + n - 1) // 512 + 1)]

        def p1load(t):
            p4 = t % NXB
            S.dma("sp", xbs[p4], x[t * 128:(t + 1) * 128, :], "x%d" % p4, writes=xbk[p4])

        def p1a(t):
            xb, hb, p2, p4 = xbs[t % NXB], hbs[t % 2], t % 2, t % NXB
            if t + 2 < 16:
                p1load(t + 2)
            ACT(junk, xb, AF.Square, xbk[p4] + ["stat"], jk + [("st", t)], accum_out=stat[:, t:t + 1])
            ACT(stat[:, 16 + t:17 + t], stat[:, t:t + 1], AF.Ln, [("st", t), "eps"], [("sd", t)],
                scale=1.0 / D, bias=eps_t[:, 0:1])
            ACT(stat[:, 32 + t:33 + t], stat[:, 16 + t:17 + t], AF.Exp, [("sd", t)], [("rs", t)], scale=-0.5)
            STT(hb, xb, stat[:, 32 + t:33 + t], gain[:], xbk[p4] + [("rs", t), "gain"], hbk[p2])

        def p1b(t):
            hb, p2 = hbs[t % 2], t % 2
            for half in range(2):
                bi = 6 + half
                bank = pb[bi][:].bitcast(BF16)
                for c in range(8):
                    k = half * 8 + c
                    TR(bank[:, c * 128:(c + 1) * 128], hb[:, k * 128:(k + 1) * 128], hbk[p2], [PS(bi)], inc=(c == 7))
                dst = hT[:, half * 8:(half + 1) * 8, t * 128:(t + 1) * 128]
                src = bank.rearrange("p (c n) -> p c n", c=8)
                wr = [("hT", half * 8 + c, t // 4) for c in range(8)]
                if half == 0:
                    ACT(dst, src, AF.Copy, [PS(bi)], wr)
                else:
                    CP(dst, src, [PS(bi)], wr)

        p1load(0)
        p1load(1)
        p1a(0)
        p1_grp = {1: [], 2: [], 3: []}
        for t in range(16):
            if t < 4:
                p1a(t + 1)
                p1b(t)
            else:
                grp = p1_grp[t // 4]
                if t + 1 < 16:
                    grp.append(partial(p1a, t + 1))
                grp.append(partial(p1b, t))

        def proj_fm(wslot, wkey, nk, src, srckeys, slices, banks):
            for k in range(nk):
                for (t0, n), bi in zip(slices, banks):
                    sk = hk(k, t0, n) if srckeys is None else srckeys
                    MM(pb[bi][:, 0:n], wslot[:, k, :], src[:, k, t0:t0 + n], (k == 0), (k == nk - 1),
                       [wkey] + sk, [PS(bi)])

        hTkeys = None

        kaTs = [ab(0, 1024), ab(1024, 2048)]
        vTst = ab(2048, 3072)
        vatoks = [ab(3072, 4096), ab(4096, 5120)]
        qaTs = [ab(5120, 5632), ab(5632, 6144)]
        sgas = [af(6144, 7168), af(7168, 8192)]
        sq = af(8192, 8704)
        stdt = af(8704, 9216)
        qn = af(9216, 9728)
        t1 = af(9728, 10240)
        t2 = af(10240, 10752)
        pbufs = [ab(10752 + 256 * i, 11008 + 256 * i) for i in range(4)]
        rec = af(11776, 12288)
        tmp = af(12288, 12800)

        sqb = sq.bitcast(BF16)[:, 0:512]
        hiA = ab(12800, 13056)
        loA = ab(13056, 13312)

        def qk_norm_rope_A(bi, gcol, t0, dst, dstkey):
            ps = pb[bi][:]
            ACT(sqb, ps, AF.Square, [PS(bi)], ["sq"])
            MM(pb[2][:], ones_bf, sqb, True, True, ["sq", "cbf"], [PS(2)])
            ACT(stdt, pb[2][:], AF.Ln, [PS(2), "eps"], ["stdt"], scale=1.0 / 128, bias=eps_t[:, 0:1])
            ACT(stdt, stdt, AF.Exp, ["stdt"], ["stdt"], scale=-0.5)
            STT(qn, ps, qkg[:, gcol:gcol + 1], stdt, [PS(bi), "qkg", "stdt"], ["qn"])
            MM(pb[3][:], permA, qn, True, True, ["qn", "cf"], [PS(3)])
            TT(t1, qn, rope[:, 0, t0:t0 + 512], ALU.mult, ["qn", "rope"], ["t1"])
            TT(t2, pb[3][:], rope[:, 1, t0:t0 + 512], ALU.mult, [PS(3), "rope"], ["t2"])
            TT(dst, t1, t2, ALU.add, ["t1", "t2"], [dstkey])

        def interleave(main, side):
            nm, ns = len(main), len(side)
            j = 0
            for i, f in enumerate(main):
                f()
                tgt = ((i + 1) * ns) // nm
                while j < min(tgt, ns):
                    side[j]()
                    j += 1
            while j < ns:
                side[j]()
                j += 1

        def nop():
            pass

        wkv = {}
        for kvh in range(2):
            wkv[("ka", kvh)] = W(("ka", kvh), 4)
            wkv[("va", kvh)] = W(("va", kvh), 4)

        def kv_proj(tag, sl, banks, k0, k1):
            wslot, wkey = wkv[tag]
            for k in range(k0, k1):
                for (t0, n), bi in zip(sl, banks):
                    MM(pb[bi][:, 0:n], wslot[:, k, :], hT[:, k, t0:t0 + n], (k == 0), (k == 15), [wkey] + hk(k, t0, n), [PS(bi)])

        def kv_proj1(tag, t0, bi, k0, k1):
            wslot, wkey = wkv[tag]
            for k in range(k0, k1):
                MM(pb[bi][:, 0:512], wslot[:, k, :], hT[:, k, t0:t0 + 512], (k == 0), (k == 15), [wkey] + hk(k, t0, 512), [PS(bi)])

        def k_units(sl_idx):
            th = []
            t0 = sl_idx * 512
            for kvh in range(2):
                kaT, kkey = kaTs[kvh], ("kaT", kvh)
                bi = kvh
                for k0 in range(0, 16, 4):
                    th.append(partial(kv_proj1, ("ka", kvh), t0, bi, k0, k0 + 4))

                def k_sq(bi=bi):
                    ACT(sqb, pb[bi][:], AF.Square, [PS(bi)], ["sq"])

                def k_ssb():
                    MM(pb[2][:], ones_bf, sqb, True, True, ["sq", "cbf"], [PS(2)])

                def k_sqrt():
                    ACT(stdt, pb[2][:], AF.Ln, [PS(2), "eps"], ["stdt"], scale=1.0 / 128, bias=eps_t[:, 0:1])

                def k_rcp():
                    ACT(stdt, stdt, AF.Exp, ["stdt"], ["stdt"], scale=-0.5)

                def k_stt(bi=bi):
                    STT(qn, pb[bi][:], qkg[:, 1:2], stdt, [PS(bi), "qkg", "stdt"], ["qn"])

                def k_hi():
                    CP(hiA, qn, ["qn"], ["hiA"])

                def k_lo():
                    S.op("dve", "tensor_tensor", dict(out=loA, in0=qn, in1=hiA, op=ALU.subtract), ["qn", "hiA"], ["loA"])

                def k_swap():
                    MM(pb[3][:], permA_bf, hiA, True, False, ["hiA", "cbf"], [PS(3)])
                    MM(pb[3][:], permA_bf, loA, False, True, ["loA", "cbf"], [PS(3)])
                    TT(t1, qn, rope[:, 0, t0:t0 + 512], ALU.mult, ["qn", "rope"], ["t1"])

                def k_t2():
                    TT(t2, pb[3][:], rope[:, 1, t0:t0 + 512], ALU.mult, [PS(3), "rope"], ["t2"])

                def k_add(kaT=kaT, kkey=kkey):
                    TT(kaT[:, t0:t0 + 512], t1, t2, ALU.add, ["t1", "t2"], [kkey])
                th += [k_sq, k_ssb, k_sqrt, k_rcp, k_stt, k_hi, k_lo, k_swap, k_t2, k_add]
            return th

        vTsts = [vTst, af(7168, 8192).bitcast(BF16)]

        def v_units(sl_idx):
            th = []
            t0 = sl_idx * 512
            for kvh in range(2):
                vst, vsk = vTsts[kvh], ("vTst", kvh)
                bi = 4 + kvh
                for k0 in range(0, 16, 4):
                    th.append(partial(kv_proj1, ("va", kvh), t0, bi, k0, k0 + 4))

                def v_ev(bi=bi, vst=vst, vsk=vsk):
                    ACT(vst[:, t0:t0 + 512], pb[bi][:], AF.Copy, [PS(bi)], [vsk])
                th.append(v_ev)
            return th

        def v_transposes():
            th = []
            for kvh in range(2):
                vatok, vkey, vst, vsk = vatoks[kvh], ("vatok", kvh), vTsts[kvh], ("vTst", kvh)
                for half in range(2):
                    def v_tr(half=half, vatok=vatok, vkey=vkey, vst=vst, vsk=vsk):
                        bank = pb[6 + half][:].bitcast(BF16)
                        for c in range(8):
                            ch = half * 8 + c
                            TR(bank[:, c * 128:(c + 1) * 128], vst[:, ch * 128:(ch + 1) * 128], [vsk], [PS(6 + half)], inc=(c == 7))
                        CP(vatok[:, half * 1024:(half + 1) * 1024], bank, [PS(6 + half)], [vkey])
                    th.append(v_tr)
            return th

        def merge2(a, b):
            out = []
            a, b = list(a), list(b)
            while a or b:
                if a:
                    out.append(a.pop(0))
                if b:
                    out.append(b.pop(0))
            return out

        for sidx in range(3):
            interleave(p1_grp[sidx + 1], merge2(k_units(sidx), v_units(sidx)))
        stop_here("p1", hT[:].rearrange("p k n -> p (k n)"), 32768)
        S.alias(("qaT", 0), ["xb2"])
        S.alias(("qaT", 1), ["xb2"])
        S.alias(("sga", 0), ["xb2"])
        for i in range(4):
            S.alias(("pbuf", i), ["xb3"])
        S.alias("rec", ["xb3"])
        S.alias("tmp", ["xb3"])
        PREP0 = []


        slA = [(0, 512), (512, 512)]

        def prepA(h):
            th = []
            qaT, sga = qaTs[h % 2], sgas[h % 2]
            qkey, gkey = ("qaT", h % 2), ("sga", h % 2)
            st = {}

            def getw(tag):
                st[tag] = W(tag, 3) if h == 0 else W(tag)

            def proj_part(tag, k0, k1):
                wslot, wkey = st[tag]
                for k in range(k0, k1):
                    for (t0, n), bi in zip(slA, [0, 1]):
                        MM(pb[bi][:, 0:n], wslot[:, k, :], hT[:, k, t0:t0 + n], (k == 0), (k == 15), [wkey] + hk(k, t0, n), [PS(bi)])

            def c_sq(bi):
                ACT(sqb, pb[bi][:], AF.Square, [PS(bi)], ["sq"])

            def c_ssb():
                MM(pb[2][:], ones_bf, sqb, True, True, ["sq", "cbf"], [PS(2)])

            def c_sqrt():
                ACT(stdt, pb[2][:], AF.Ln, [PS(2), "eps"], ["stdt"], scale=1.0 / 128, bias=eps_t[:, 0:1])

            def c_rcp():
                ACT(stdt, stdt, AF.Exp, ["stdt"], ["stdt"], scale=-0.5)

            def c_stt(bi):
                STT(qn, pb[bi][:], qkg[:, 0:1], stdt, [PS(bi), "qkg", "stdt"], ["qn"])

            def c_hi():
                CP(hiA, qn, ["qn"], ["hiA"])

            def c_lo():
                S.op("dve", "tensor_tensor", dict(out=loA, in0=qn, in1=hiA, op=ALU.subtract), ["qn", "hiA"], ["loA"])

            def c_swap(t0):
                MM(pb[2][:], permA_bf, hiA, True, False, ["hiA", "cbf"], [PS(2)])
                MM(pb[2][:], permA_bf, loA, False, True, ["loA", "cbf"], [PS(2)])
                TT(t1, qn, rope[:, 0, t0:t0 + 512], ALU.mult, ["qn", "rope"], ["t1"])

            def c_t2(t0):
                TT(t2, pb[2][:], rope[:, 1, t0:t0 + 512], ALU.mult, [PS(2), "rope"], ["t2"])

            def c_add(t0):
                TT(qaT[:, t0:t0 + 512], t1, t2, ALU.add, ["t1", "t2"], [qkey])

            def silu(t0, bi):
                dst = sga[:, t0:t0 + 512]
                ACT(dst, pb[bi][:], AF.Exp, [PS(bi)], [gkey], scale=-1.0)
                ACT(dst, dst, AF.Ln, [gkey, "one"], [gkey], bias=one_t[:, 0:1])
                ACT(dst, dst, AF.Exp, [gkey], [gkey], scale=-1.0)
                TT(dst, dst, pb[bi][:], ALU.mult, [gkey, PS(bi)], [gkey])

            th.append(partial(getw, ("qa", h)))
            for k0 in range(0, 16, 2):
                th.append(partial(proj_part, ("qa", h), k0, k0 + 2))
            for (t0, n), bi in zip(slA, [0, 1]):
                th += [partial(c_sq, bi), c_ssb, c_sqrt, nop, c_rcp, nop, nop, partial(c_stt, bi), nop,
                       c_hi, c_lo, nop, partial(c_swap, t0), nop, partial(c_t2, t0), nop, partial(c_add, t0)]
            th.append(partial(getw, ("ga", h)))
            for k0 in range(0, 16, 2):
                th.append(partial(proj_part, ("ga", h), k0, k0 + 2))
            th.append(nop)
            for (t0, n), bi in zip(slA, [0, 1]):
                th.append(partial(silu, t0, bi))
            return th

        def attnA(h):
            th = []
            kvh = h // 4
            kaT, vatok = kaTs[kvh], vatoks[kvh]
            kkey, vkey = ("kaT", kvh), ("vatok", kvh)
            qaT, sga = qaTs[h % 2], sgas[h % 2]
            qkey, gkey = ("qaT", h % 2), ("sga", h % 2)

            SB = [3, 4, 5]

            def s_mm(q0, kc):
                bi = SB[kc % 3]
                MM(pb[bi][:], kaT[:, kc * 128:(kc + 1) * 128], qaT[:, q0:q0 + 512], True, True, [kkey, qkey], [PS(bi)])

            def stepa(q0, kc):
                bi = SB[kc % 3]
                ACT(pbufs[kc % 4], pb[bi][:], AF.Exp, [PS(bi)], [("pbuf", kc % 4)], scale=SCALE)
                if kc + 2 < 16:
                    s_mm(q0, kc + 2)

            def stepb(q0, kc):
                pbuf, pkey = pbufs[kc % 4], ("pbuf", kc % 4)
                MM(pb[6][:], vatok[:, kc * 128:(kc + 1) * 128], pbuf, (kc == 0), (kc == 15), [vkey, pkey], [PS(6)])
                MM(pb[7][:], ones_bf, pbuf, (kc == 0), (kc == 15), ["cbf", pkey], [PS(7)])

            def epi(q0):
                ACT(rec, pb[7][:], AF.Ln, [PS(7)], ["rec"])
                CP(tmp, pb[6][:], [PS(6)], ["tmp"])
                ACT(rec, rec, AF.Exp, ["rec"], ["rec"], scale=-1.0)
                TT(tmp, tmp, rec, ALU.mult, ["tmp", "rec"], ["tmp"])
                TT(yaT[:, h, q0:q0 + 512], tmp, sga[:, q0:q0 + 512], ALU.mult, ["tmp", gkey], [("yaT", h)])

            def start(q0):
                s_mm(q0, 0)
                s_mm(q0, 1)

            for qb in range(2):
                q0 = qb * 512
                if qb == 0:
                    th.append(partial(start, q0))
                for kc in range(16):
                    th.append(partial(stepa, q0, kc))
                    if kc >= 1:
                        th.append(partial(stepb, q0, kc - 1))
                if qb == 0:
                    th.append(partial(start, 512))
                th.append(partial(stepb, q0, 15))
                th.append(partial(epi, q0))
            return th

        interleave(k_units(3) + prepA(0), v_units(3) + v_transposes())
        S.alias(("sga", 1), [("vTst", 1)])
        S.alias("vTst", [("vTst", 0)])
        for h in range(7):
            interleave(attnA(h), prepA(h + 1))
        kTs = [ab(0, 1024), ab(1024, 2048)]
        vTst = ab(2048, 3072)
        vtoks = [ab(3072, 4096), ab(4096, 5120)]
        qTs = [ab(5120, 5632), ab(5632, 6144)]
        sgb = af(6144, 7168)
        xs = af(8192, 8704)
        U = af(11776, 12800)
        L = af(7168, 8192)
        GD = (1, 4, 16)
        GRL = (1152, 384, 128)
        GPAD = (64, 64, 0)
        GKTOK = (1088, 1280, 2048)

        def slices_of(ntok):
            out = []
            t0 = 0
            while t0 < ntok:
                n = min(512, ntok - t0)
                out.append((t0, n))
                t0 += n
            return out

        def deint(buf, g, t0, n, own):
            d = GD[g]
            if own:
                rl, pad = TO // d, 0
            else:
                rl, pad = GRL[g], GPAD[g]
            if d == 1:
                return buf[:, pad + t0:pad + t0 + n]
            i0, ni = t0 // d, n // d
            return buf[:, 0:d * rl].rearrange("p (r c) -> p r c", r=d)[:, :, pad + i0:pad + i0 + ni]

        def nat(ap, g):
            d = GD[g]
            if d == 1:
                return ap
            return ap.rearrange("p (i r) -> p r i", r=d)

        def qk_rope_B(bi, g, t0, n, buf, bkey, own):
            ACT(deint(buf, g, t0, n, own), nat(pb[bi][:, 0:n], g), AF.Copy, [PS(bi)], [bkey])
            CP(xs[:, 0:n], pb[bi][:, 0:n], [PS(bi)], ["xs"])
            MM(pb[2][:, 0:n], permB, xs[:, 0:n], True, True, ["xs", "cf"], [PS(2)])
            TT(t1[0:32, 0:n], xs[0:32, 0:n], rope[0:32, 0, t0:t0 + n], ALU.mult, ["xs", "rope"], ["t1"])
            TT(t2[0:32, 0:n], pb[2][0:32, 0:n], rope[0:32, 1, t0:t0 + n], ALU.mult, [PS(2), "rope"], ["t2"])
            TT(deint(buf[0:32], g, t0, n, own), nat(t1[0:32, 0:n], g), nat(t2[0:32, 0:n], g), ALU.add,
               ["t1", "t2", bkey], [bkey])

        xs_l = [af(8192, 8704), af(8704, 9216)]
        t2_l = [af(9728, 10240), af(10240, 10752)]
        hiB = [ab(9216, 9472), ab(12800, 13056)]
        loB = [ab(9472, 9728), ab(13056, 13312)]
        PB_SW = 7
        iters = [(hs, g) for hs in range(4) for g in range(3)]

        def prepB(n, nproj=3, PB_SW=7):
            hs, g = iters[n]
            d = GD[g]
            kT, vtok, qT = kTs[n % 2], vtoks[n % 2], qTs[n % 2]
            kkey, vkey, qkey = ("kT", n % 2), ("vtok", n % 2), ("qT", n % 2)
            ksl = slices_of(GKTOK[g])
            th = []
            st = {}
            pairno = [0]

            def getw(tag):
                st[tag] = W(tag)

            def proj_part(tag, sl, banks, k0, k1):
                wslot, wkey = st[tag]
                if k0 == 0:
                    order = [(k, j) for j in range(len(sl)) for k in range(k0, k1)]
                else:
                    order = [(k, j) for k in range(k0, k1) for j in range(len(sl))]
                for k, j in order:
                    (t0, n_), bi = sl[j], banks[j]
                    MM(pb[bi][:, 0:n_], wslot[:, k, :], hT[:, k, t0:t0 + n_], (k == 0), (k == 15), [wkey] + hk(k, t0, n_), [PS(bi)])

            def ev_act(bi, t0, n_, buf, bkey, own):
                ACT(deint(buf, g, t0, n_, own), nat(pb[bi][:, 0:n_], g), AF.Copy, [PS(bi)], [bkey])

            def ev_cp(bi, n_, j):
                ACT(xs_l[j][:, 0:n_], pb[bi][:, 0:n_], AF.Copy, [PS(bi)], [("xs", j)])

            def r_hi(n_, j):
                ACT(hiB[j][:, 0:n_], xs_l[j][:, 0:n_], AF.Copy, [("xs", j)], [("hiB", j)])

            def r_lo(n_, j):
                S.op("dve", "tensor_tensor", dict(out=loB[j][:, 0:n_], in0=xs_l[j][:, 0:n_], in1=hiB[j][:, 0:n_], op=ALU.subtract),
                     [("xs", j), ("hiB", j)], [("loB", j)])

            def r_mm(n_, j):
                MM(pb[PB_SW][:, 0:n_], permB_bf, hiB[j][:, 0:n_], True, False, [("hiB", j), "cbf"], [PS(PB_SW)])
                MM(pb[PB_SW][:, 0:n_], permB_bf, loB[j][:, 0:n_], False, True, [("loB", j), "cbf"], [PS(PB_SW)])

            def r_t2(t0, n_, j):
                TT(t2_l[j][0:32, 0:n_], pb[PB_SW][0:32, 0:n_], rope[0:32, 1, t0:t0 + n_], ALU.mult, [PS(PB_SW), "rope"], [("t2", j)])

            def r_fin(t0, n_, buf, bkey, own, j):
                xs_, t2_ = xs_l[j], t2_l[j]
                TT(xs_[0:32, 0:n_], xs_[0:32, 0:n_], rope[0:32, 0, t0:t0 + n_], ALU.mult, [("xs", j), "rope"], [("xs", j)])
                TT(deint(buf[0:32], g, t0, n_, own), nat(xs_[0:32, 0:n_], g), nat(t2_[0:32, 0:n_], g), ALU.add,
                   [("xs", j), ("t2", j), bkey], [bkey])

            def evac_v(bi, t0, n_):
                ACT(deint(vTst, g, t0, n_, False), nat(pb[bi][:, 0:n_], g), AF.Copy, [PS(bi)], ["vTst"])

            def transposes(c0, cn):
                bank = pb[PB_SW][:].bitcast(BF16)
                for c in range(cn):
                    ch = c0 + c
                    TR(bank[:, c * 128:(c + 1) * 128], vTst[:, ch * 128:(ch + 1) * 128], ["vTst"], [PS(PB_SW)], inc=(c == cn - 1))
                CP(vtok[:, c0 * 128:(c0 + cn) * 128], bank[:, 0:cn * 128], [PS(PB_SW)], [vkey])

            def pads():
                if GPAD[g]:
                    MEMSET(kT[:, 0:d * GRL[g]].rearrange("p (r c) -> p r c", r=d)[:, :, 0:64], 0.0, [kkey])
                    MEMSET(vTst[:, 0:d * GRL[g]].rearrange("p (r c) -> p r c", r=d)[:, :, 0:64], 0.0, ["vTst"])

            def family(tag, slices, kind, buf, bkey, own, pending):
                th.append(partial(getw, tag))
                for i in range(0, len(slices), 2):
                    sl = slices[i:i + 2]
                    banks = [(2 * pairno[0]) % nproj, (2 * pairno[0] + 1) % nproj][:len(sl)]
                    pairno[0] += 1
                    for (k0, k1) in [(0, 4)] + [(k, k + 2) for k in range(4, 16, 2)]:
                        th.append(partial(proj_part, tag, sl, banks, k0, k1))
                        if pending:
                            th.append(pending.pop(0))
                    th.extend(pending)
                    pending = []
                    for j, ((t0, n_), bi) in enumerate(zip(sl, banks)):
                        if kind == "v":
                            th.append(partial(evac_v, bi, t0, n_))
                        else:
                            th.append(partial(ev_act, bi, t0, n_, buf, bkey, own))
                            th.append(partial(ev_cp, bi, n_, j))
                    if kind != "v":
                        for j, ((t0, n_), bi) in enumerate(zip(sl, banks)):
                            pending.append(partial(r_hi, n_, j))
                        for j, ((t0, n_), bi) in enumerate(zip(sl, banks)):
                            pending.append(partial(r_lo, n_, j))
                        for j, ((t0, n_), bi) in enumerate(zip(sl, banks)):
                            pending.append(partial(r_mm, n_, j))
                            pending.append(partial(r_t2, t0, n_, j))
                        for j, ((t0, n_), bi) in enumerate(zip(sl, banks)):
                            pending.append(partial(r_fin, t0, n_, buf, bkey, own, j))
                return pending

            th.append(pads)
            pend = family(("kb", g, hs), ksl, "k", kT, kkey, False, [])
            pend = family(("vb", g, hs), ksl, "v", None, None, False, pend)
            nch = d * GRL[g] // 128
            pend = family(("qb", g, hs), [(0, 512), (512, 512)], "q", qT, qkey, True, pend)
            for c0 in range(0, nch, 8):
                th.append(partial(transposes, c0, min(8, nch - c0)))
                if pend:
                    th.append(pend.pop(0))
                if pend:
                    th.append(pend.pop(0))
            th.extend(pend)
            if g == 2:
                def gbproj():
                    wslot, wkey = W(("gb", hs))
                    sl = [(0, 512), (512, 512)]
                    proj_fm(wslot, wkey, 16, hT, hTkeys, sl, [0, 1])
                    for (t0, n_), bi in zip(sl, [0, 1]):
                        dst = sgb[:, t0:t0 + 512]
                        ACT(dst, pb[bi][:], AF.Exp, [PS(bi)], ["sgb"], scale=-1.0)
                        ACT(dst, dst, AF.Ln, ["sgb", "one"], ["sgb"], bias=one_t[:, 0:1])
                        ACT(dst, dst, AF.Exp, ["sgb"], ["sgb"], scale=-1.0)
                        TT(dst, dst, pb[bi][:], ALU.mult, ["sgb", PS(bi)], ["sgb"])
                th.append(gbproj)
            return th

        pbufsB = [ab(10752 + 128 * i, 10752 + 128 * (i + 1)) for i in range(8)]

        def attnB(n):
            hs, g = iters[n]
            kT, vtok, qT = kTs[n % 2], vtoks[n % 2], qTs[n % 2]
            kkey, vkey, qkey = ("kT", n % 2), ("vtok", n % 2), ("qT", n % 2)
            LAG = 3
            iss = []
            cons = []
            P = {}

            def issue(idx, kcol, qcol, n_, mask):
                i = idx % 8
                bi = 4 + (idx % 2)
                pbuf, pkey = pbufsB[i], ("pbufB", i)
                MM(pb[bi][:, 0:n_], kT[:, kcol:kcol + 128], qT[:, qcol:qcol + n_], True, True, [kkey, qkey], [PS(bi)])
                ACT(pbuf[:, 0:n_], pb[bi][:, 0:n_], AF.Exp, [PS(bi)], [pkey], scale=SCALE)
                TT(pbuf[:, 0:n_], pbuf[:, 0:n_], mask, ALU.mult, [pkey, "cbf"], [pkey])
                P[idx] = (pbuf, pkey)

            def consume(ob, ocol, n_, parts):
                for j, (vch, idx, pc) in enumerate(parts):
                    pbuf, pkey = P[idx]
                    MM(pb[ob][:, ocol:ocol + n_], vtok[:, vch * 128:(vch + 1) * 128], pbuf[:, pc:pc + n_],
                       (j == 0), (j == len(parts) - 1), [vkey, pkey], [PS(ob)])
                for j, (vch, idx, pc) in enumerate(parts):
                    pbuf, pkey = P[idx]
                    MM(pb[ob][:, 256 + ocol:256 + ocol + n_], ones_bf, pbuf[:, pc:pc + n_],
                       (j == 0), (j == len(parts) - 1), ["cbf", pkey], [PS(ob)])

            def combine(qq, ob):
                for (acc, akey, c0) in ((U, "U", 0), (L, "L", 256)):
                    src = pb[ob][:, c0:c0 + 256]
                    if g == 0:
                        CP(acc[:, qq * 256:(qq + 1) * 256], src, [PS(ob)], [akey])
                    elif g == 1:
                        av = acc.rearrange("p (i r) -> p r i", r=4)[:, qq, :]
                        TT(av, src, av, ALU.add, [PS(ob), akey], [akey])
                    else:
                        av = acc.rearrange("p (i r) -> p r i", r=16)[:, 4 * qq:4 * qq + 4, :]
                        TT(av, src.rearrange("p (r i) -> p r i", r=4), av, ALU.add, [PS(ob), akey], [akey])

            def epilogue():
                RCP(L, L, ["L"], ["L"])
                TT(U, U, L, ALU.mult, ["U", "L"], ["U"])
                TT(ybT[:, hs, :], U, sgb, ALU.mult, ["U", "sgb"], [("ybT", hs)])

            idx = 0
            for qq in range(4):
                ob = 6 if qq % 2 == 0 else 3
                if g == 2:
                    for rr in range(4):
                        r = 4 * qq + rr
                        iss.append(partial(issue, idx, r * 128, r * 64, 64, mask2))
                        cons.append((idx, partial(consume, ob, rr * 64, 64, [(r, idx, 0)])))
                        idx += 1
                else:
                    if g == 0:
                        kbase, qbase, jf, nb, seqstart, vbase = 0, 0, 2 * qq, 2, (qq == 0), 0
                    else:
                        kbase, qbase, jf, nb, seqstart, vbase = qq * 384, qq * 256, 0, 2, True, qq * 3
                    prev = None
                    for j in range(jf, jf + nb + 1):
                        if j == jf:
                            iss.append(partial(issue, idx, kbase + 128 * j, qbase + 128 * j, 128,
                                               mask0 if seqstart else maskB[:, 128:256]))
                            up = 0
                        elif j == jf + nb:
                            iss.append(partial(issue, idx, kbase + 128 * j, qbase + 128 * (j - 1), 128, maskB[:, 0:128]))
                            up = None
                        else:
                            iss.append(partial(issue, idx, kbase + 128 * j, qbase + 128 * (j - 1), 256, maskB))
                            up = 128
                        if prev is not None:
                            cons.append((idx, partial(consume, ob, (j - 1 - jf) * 128, 128,
                                                      [(vbase + j - 1, prev[0], prev[1]), (vbase + j, idx, 0)])))
                        prev = (idx, up)
                        idx += 1
                cons.append((idx - 1, partial(combine, qq, ob)))
            if g == 2:
                cons.append((idx - 1, epilogue))
            th = []
            ci = 0
            for i, f in enumerate(iss):
                th.append(f)
                while ci < len(cons) and cons[ci][0] <= i - LAG:
                    th.append(cons[ci][1])
                    ci += 1
            while ci < len(cons):
                th.append(cons[ci][1])
                ci += 1
            return th

        S.alias(("kT", 0), [("kaT", 0)])
        S.alias(("vtok", 0), [("vatok", 0)])
        S.alias(("qT", 0), [("qaT", 0)])
        S.alias(("xs", 0), ["sq"])
        S.alias(("xs", 1), ["stdt"])
        S.alias(("t2", 0), ["t1"])
        S.alias(("t2", 1), ["t2"])
        S.alias(("hiB", 0), ["qn"])
        S.alias(("loB", 0), ["qn"])
        S.alias(("hiB", 1), ["hiA"])
        S.alias(("loB", 1), ["loA"])
        S.dma("sp", rope[0:32, 0, :], ropeB_d[0], "c_rope", writes=["rope"])
        S.dma("sp", rope[0:32, 1, :], ropeB_d[1], "c_rope", writes=["rope"])
        interleave(attnA(7), prepB(0, nproj=2, PB_SW=2))
        S.alias(("kT", 1), [("kaT", 1)])
        S.alias(("vtok", 1), [("vatok", 1)])
        S.alias(("qT", 1), [("qaT", 1)])
        for i in range(8):
            S.alias(("pbufB", i), [("pbuf", i // 2)])
        S.alias("U", ["rec", "tmp"])
        S.alias("L", [("sga", 1)])
        S.alias("sgb", [("sga", 0)])
        for n in range(12):
            interleave(prepB(n + 1) if n < 11 else [], attnB(n)) if n < 11 else [f() for f in attnB(n)]
        stop_here("B", ybT[:].rearrange("p k n -> p (k n)"), 4096)
        S.barrier()

        wbig = [wsl[:, 4 * j:4 * j + 4].rearrange("p s k c -> p (s k c)").rearrange("p (k n) -> p k n", k=16) for j in range(2)]
        wrope = rope[:].rearrange("p a n -> p (a n)").bitcast(BF16).rearrange("p (k n) -> p k n", k=16)

        def bigbuf(cb):
            if cb in (0, 3):
                return wrope, ["rope"]
            j = cb - 1
            return wbig[j], [("w", 4 * j + i) for i in range(4)]

        def big_issue(cb):
            wv, wk = bigbuf(cb)
            src = wout_d[:, cb * 512:(cb + 1) * 512].rearrange("(k p) c -> p k c", p=128)
            S.dma("pool", wv, src, "wbig%d" % cb, writes=wk)

        big_issue(0)
        mergedT = ab(0, 8192).rearrange("p (k t) -> p k t", k=16)
        mw = [[af(8192 + (s * 4 + i) * 512, 8192 + (s * 4 + i + 1) * 512) for i in range(4)] for s in range(2)]
        yakeys = [("yaT", h) for h in range(8)]
        ybkeys = [("ybT", h) for h in range(4)]
        for c in range(16):
            wza, kza = W(("za", c), 4)
            wzb, kzb = W(("zb", c), 4)
            wa, kwa = W(("wba", c), 4)
            wb_, kwb = W(("wbb", c), 4)
            for s in range(2):
                t0 = s * 512
                b0 = 4 * s
                proj_fm(wza, kza, 16, hT, hTkeys, [(t0, 512)], [b0])
                proj_fm(wzb, kzb, 16, hT, hTkeys, [(t0, 512)], [b0 + 1])
                proj_fm(wa, kwa, 8, yaT, yakeys, [(t0, 512)], [b0 + 2])
                proj_fm(wb_, kwb, 4, ybT, ybkeys, [(t0, 512)], [b0 + 3])
                sa, sbb, m1, m2 = mw[s]
                ACT(sa, pb[b0][:], AF.Sigmoid, [PS(b0), "mb"], [("sa", s)], bias=mb[:, c:c + 1])
                ACT(sbb, pb[b0 + 1][:], AF.Sigmoid, [PS(b0 + 1), "mb"], [("sb", s)], bias=mb[:, 16 + c:17 + c])
                TT(m1, pb[b0 + 2][:], sa, ALU.mult, [PS(b0 + 2), ("sa", s)], [("m1", s)])
                TT(m2, pb[b0 + 3][:], sbb, ALU.mult, [PS(b0 + 3), ("sb", s)], [("m2", s)])
                TT(mergedT[:, c, t0:t0 + 512], m1, m2, ALU.add, [("m1", s), ("m2", s)], [("mT", c)])

        stop_here("M", ab(0, 8192), 16384)
        S.dma("sp", gain[:], gf_d, "c_gain", writes=["gain"])
        xflat = hT[:].bitcast(F32).rearrange("p k n -> p (k n)")
        for t in range(8):
            S.dma("sp", xflat[:, t * 2048:(t + 1) * 2048], x[t * 128:(t + 1) * 128, :], "xr%d" % t,
                  writes=hk(2 * t) + hk(2 * t + 1))
        big_issue(1)
        big_issue(2)
        mkeys = [("mT", c) for c in range(16)]
        junk2 = ab(8192, 9216)
        ost = [af(9216, 11264), af(11264, 13312)]
        otoks = []

        def final_norm(t):
            xt = xflat[:, t * 2048:(t + 1) * 2048]
            xkeys = hk(2 * t) + hk(2 * t + 1)
            o = ost[t % 2]
            ACT(junk2, xt, AF.Square, xkeys + ["stat"], ["junk2", ("fs", t)], accum_out=stat[:, 48 + t:49 + t])
            ACT(stat[:, 56 + t:57 + t], stat[:, 48 + t:49 + t], AF.Ln, [("fs", t), "eps"], [("fd", t)],
                scale=1.0 / D, bias=eps_t[:, 0:1])
            ACT(stat[:, 56 + t:57 + t], stat[:, 56 + t:57 + t], AF.Exp, [("fd", t)], [("fd", t)], scale=-0.5)
            STT(o, xt, stat[:, 56 + t:57 + t], gain[:], xkeys + [("fd", t), "gain"], [("ost", t % 2)])
            otoks.append(S.dma("sp", y[t * 128:(t + 1) * 128, :], o, "out%d" % (t % 2), reads=[("ost", t % 2)]))

        cnt = 0
        for cb in range(4):
            wv, wk = bigbuf(cb)
            for t in range(8):
                bi = cnt % 8
                cnt += 1
                for k in range(16):
                    MM(pb[bi][:], mergedT[:, k, t * 128:(t + 1) * 128], wv[:, k, :], (k == 0), (k == 15),
                       mkeys + wk, [PS(bi)])
                xv = xflat[:, t * 2048 + cb * 512:t * 2048 + (cb + 1) * 512]
                xk = hk(2 * t + cb // 2)
                TT(xv, pb[bi][:], xv, ALU.add, [PS(bi)] + xk, xk)
                if cb == 3:
                    final_norm(t)
            if cb == 0:
                big_issue(3)
        S.wait_token("sp", otoks[-1])
        S.wait_token("sp", otoks[-2])
        print("[sched] ops per engine:", S.check())
        with nc.Block() as block:
            S.emit(block)
    return nc


def _rope_tables(tok):
    tok = np.asarray(tok)
    row = (tok // 64).astype(np.float32)
    col = (tok % 64).astype(np.float32)
    pos = tok.astype(np.float32)

    def angles(p, dim, theta):
        expo = np.arange(0, dim, 2, dtype=np.float32) / np.float32(dim)
        inv = (np.float32(1.0) / np.power(np.float32(theta), expo)).astype(np.float32)
        ang = (p[:, None] * inv[None, :]).astype(np.float32)
        return np.cos(ang).astype(np.float32), np.sin(ang).astype(np.float32)

    cr, sr = angles(row, 64, 10000.0)
    cc, sc = angles(col, 64, 10000.0)
    CA = np.concatenate([cr, cr, cc, cc], axis=1).T
    SA = np.concatenate([-sr, sr, -sc, sc], axis=1).T
    cb, sbb = angles(pos, 32, 500000.0)
    CB = np.concatenate([cb, cb], axis=1).T
    SB = np.concatenate([-sbb, sbb], axis=1).T
    return (np.ascontiguousarray(np.stack([CA, SA])).astype(np.float32),
            np.ascontiguousarray(np.stack([CB, SB])).astype(np.float32))


def _consts():
    cbf = np.zeros((128, 1088), np.float32)
    cbf[:, 0:128] = np.eye(128, dtype=np.float32)
    cbf[:, 128:256] = 1.0
    kk = np.arange(128)[:, None]
    qq = np.arange(256)[None, :]
    band = ((kk <= qq) & (qq <= kk + 128)).astype(np.float32)
    cbf[:, 384:640] = band
    cbf[:, 640:768] = band[:, 128:256] * (kk >= 64)
    q64 = np.arange(64)[None, :]
    cbf[:, 768:832] = (np.abs(kk - q64) <= 64).astype(np.float32)
    cf = np.zeros((128, 384), np.float32)
    for m in range(128):
        cf[m + 32 if (m // 32) % 2 == 0 else m - 32, m] = 1.0
    for m in range(32):
        cf[m + 16 if m < 16 else m - 16, 128 + m] = 1.0
    cf[:, 256:384] = 1.0 / 128.0
    cbf[:, 832:960] = cf[:, 0:128]
    cbf[:, 960:1088] = cf[:, 128:256]
    return cbf, cf


_NC_CACHE = {}


def kernel(x, norm_gain, w_in, q_norm_gain, k_norm_gain, merge_gate_bias,
           w_branch_a, w_branch_b, w_out, final_norm_gain, _debug=False):
    x = np.asarray(x, np.float32)
    key = bool(_debug)
    if key not in _NC_CACHE:
        _NC_CACHE[key] = build_nc(debug=key)
    nc = _NC_CACHE[key]
    cbf, cf = _consts()
    shared = {
        "w_in": np.ascontiguousarray(np.asarray(w_in, np.float32)[0]),
        "wba": np.ascontiguousarray(np.asarray(w_branch_a, np.float32)[0]),
        "wbb": np.ascontiguousarray(np.asarray(w_branch_b, np.float32)[0]),
        "wout": np.ascontiguousarray(np.asarray(w_out, np.float32)[0]),
        "gn": np.ascontiguousarray(np.broadcast_to(np.asarray(norm_gain, np.float32)[0][None, :], (128, D))),
        "gf": np.ascontiguousarray(np.broadcast_to(np.asarray(final_norm_gain, np.float32)[None, :], (128, D))),
        "qkg": np.ascontiguousarray(np.stack([np.asarray(q_norm_gain, np.float32)[0],
                                              np.asarray(k_norm_gain, np.float32)[0]], axis=1)),
        "mbias": np.ascontiguousarray(np.asarray(merge_gate_bias, np.float32)[0].reshape(2, 16, 128).transpose(2, 0, 1).reshape(128, 32)),
        "cbf": cbf, "cf": cf,
    }
    in_maps = []
    perms = []
    for c in range(8):
        b, half = c // 2, c % 2
        tok = np.arange(T) if half == 0 else (T - 1 - np.arange(T))
        perms.append(tok)
        ra, rb = _rope_tables(tok)
        m = dict(shared)
        m["x"] = np.ascontiguousarray(x[b][tok])
        m["ropeA"] = ra
        m["ropeB"] = rb
        in_maps.append(m)
    res = run_bass_kernel_spmd(nc, in_maps, core_ids=list(range(8)))
    out = np.empty((4, T, D), np.float32)
    for c in range(8):
        b = c // 2
        out[b, perms[c][:TO]] = np.asarray(res.results[c]["y"], np.float32)
    if _debug:
        return out, res
    return out
```
